# Optimizing a Trainium2 kernel written in Bass

```python
import jax, jax.numpy as jnp
from jax import lax
import numpy as np

D_MODEL = 1024
BATCH = 8
SEQ = 2048
DEPTH = 1

CHUNK = 64
GLA_HEADS = 4
GLA_KEY_DIM = D_MODEL // 2
GLA_VAL_DIM = D_MODEL
GLA_DK = GLA_KEY_DIM // GLA_HEADS
GLA_DV = GLA_VAL_DIM // GLA_HEADS
GLA_GATE_RANK = 16
GLA_GATE_TAU = 16.0
CONV_WIDTH = D_MODEL
CONV_KERNEL = 31
FFN_HIDDEN = -(-8 * D_MODEL // (3 * 256)) * 256
NORM_EPS = 1e-6
IN_SIZES = (GLA_KEY_DIM, GLA_KEY_DIM, GLA_VAL_DIM, GLA_VAL_DIM, GLA_GATE_RANK,
            2 * CONV_WIDTH, D_MODEL, D_MODEL)
IN_COLS = GLA_KEY_DIM * 2 + GLA_VAL_DIM * 2 + GLA_GATE_RANK + 2 * CONV_WIDTH + 2 * D_MODEL

kernel_name = "hybrid_gla_conformer_conv_gated_block"


def rms_norm(x, g):
    xf = x.astype(jnp.float32)
    y = xf * lax.rsqrt(jnp.mean(xf * xf, axis=-1, keepdims=True) + NORM_EPS)
    return (y * g.astype(jnp.float32)).astype(x.dtype)


def layer_norm(x, g, b):
    xf = x.astype(jnp.float32)
    mu = jnp.mean(xf, axis=-1, keepdims=True)
    xc = xf - mu
    var = jnp.mean(xc * xc, axis=-1, keepdims=True)
    y = xc * lax.rsqrt(var + NORM_EPS)
    return (y * g.astype(jnp.float32) + b.astype(jnp.float32)).astype(x.dtype)


def gla_chunked(q, k, v, log_a):
    B, S = q.shape[0], q.shape[1]
    N = S // CHUNK

    def to_chunks(t, d):
        return t.astype(jnp.float32).reshape(B, N, CHUNK, GLA_HEADS, d).transpose(0, 3, 1, 2, 4)

    qc = to_chunks(q, GLA_DK) * (GLA_DK ** -0.5)
    kc = to_chunks(k, GLA_DK)
    vc = to_chunks(v, GLA_DV)
    gc = to_chunks(log_a, GLA_DK)

    L = jnp.cumsum(gc, axis=3)
    L_last = L[:, :, :, -1:, :]
    q_dec = qc * jnp.exp(L)
    k_dec = kc * jnp.exp(-L)

    scores = jnp.einsum('bhnid,bhnjd->bhnij', q_dec, k_dec)
    causal = jnp.tril(jnp.ones((CHUNK, CHUNK), dtype=bool))
    scores = jnp.where(causal, scores, 0.0)
    o_intra = jnp.einsum('bhnij,bhnje->bhnie', scores, vc)

    k_to_end = kc * jnp.exp(L_last - L)
    U = jnp.einsum('bhncd,bhnce->nbhde', k_to_end, vc)
    chunk_decay = jnp.exp(L_last[:, :, :, 0, :]).transpose(2, 0, 1, 3)

    def step(state, inp):
        u, a = inp
        return a[..., None] * state + u, state

    init = jnp.zeros((B, GLA_HEADS, GLA_DK, GLA_DV), jnp.float32)
    _, S_prev = lax.scan(step, init, (U, chunk_decay))
    o_inter = jnp.einsum('bhncd,nbhde->bhnce', q_dec, S_prev)

    o = o_intra + o_inter
    return o.transpose(0, 2, 3, 1, 4).reshape(B, S, GLA_HEADS, GLA_DV)


def gla_branch(q, k, v, r, a_lr, w_alpha_up, b_alpha, gla_norm, w_gla_out):
    B, S = q.shape[0], q.shape[1]
    z = (a_lr @ w_alpha_up + b_alpha).astype(jnp.float32)
    log_a = jax.nn.log_sigmoid(z) / GLA_GATE_TAU
    shp_k = (B, S, GLA_HEADS, GLA_DK)
    o = gla_chunked(q.reshape(shp_k), k.reshape(shp_k),
                    v.reshape(B, S, GLA_HEADS, GLA_DV), log_a.reshape(shp_k))
    o = o * lax.rsqrt(jnp.mean(o * o, axis=-1, keepdims=True) + NORM_EPS)
    o = o * gla_norm.astype(jnp.float32).reshape(GLA_HEADS, GLA_DV)
    o = o.reshape(B, S, GLA_VAL_DIM).astype(q.dtype) * jax.nn.silu(r)
    return o @ w_gla_out


def conv_branch(u_glu, conv_w, conv_b, conv_ln_g, conv_ln_b, w_conv_out):
    a, b = jnp.split(u_glu, 2, axis=-1)
    z = a * jax.nn.sigmoid(b)
    z = jnp.pad(z, ((0, 0), (CONV_KERNEL - 1, 0), (0, 0)))
    z = lax.conv_general_dilated(z, conv_w[:, None, :], window_strides=(1,), padding='VALID',
                                 dimension_numbers=('NWC', 'WIO', 'NWC'),
                                 feature_group_count=CONV_WIDTH) + conv_b
    z = jax.nn.silu(layer_norm(z, conv_ln_g, conv_ln_b))
    return z @ w_conv_out


def setup_inputs(seed: int = 0) -> dict:
    key = jax.random.key(seed)
    ks = jax.random.split(key, 24)
    f32 = jnp.float32

    def w(k, shape, fan_in):
        return jax.random.normal(k, shape, f32) * (fan_in ** -0.5)

    def gain(k, shape):
        return 1.0 + 0.02 * jax.random.normal(k, shape, f32)

    def bias(k, shape, s=0.02):
        return s * jax.random.normal(k, shape, f32)

    L = DEPTH
    return {
        "x": jax.random.normal(ks[0], (BATCH, SEQ, D_MODEL), f32),
        "norm_mix_pre": gain(ks[1], (L, D_MODEL)),
        "w_in": w(ks[2], (L, D_MODEL, IN_COLS), D_MODEL),
        "w_alpha_up": w(ks[3], (L, GLA_GATE_RANK, GLA_KEY_DIM), GLA_GATE_RANK),
        "b_alpha": bias(ks[4], (L, GLA_KEY_DIM), 0.1),
        "gla_norm": gain(ks[5], (L, GLA_VAL_DIM)),
        "w_gla_out": w(ks[6], (L, GLA_VAL_DIM, D_MODEL), GLA_VAL_DIM),
        "conv_w": w(ks[7], (L, CONV_KERNEL, CONV_WIDTH), CONV_KERNEL),
        "conv_b": bias(ks[8], (L, CONV_WIDTH)),
        "conv_ln_g": gain(ks[9], (L, CONV_WIDTH)),
        "conv_ln_b": bias(ks[10], (L, CONV_WIDTH)),
        "w_conv_out": w(ks[11], (L, CONV_WIDTH, D_MODEL), CONV_WIDTH),
        "w_out": w(ks[12], (L, D_MODEL, D_MODEL), D_MODEL),
        "norm_mix_post": gain(ks[13], (L, D_MODEL)),
        "norm_ffn_pre": gain(ks[14], (L, D_MODEL)),
        "w_ffn_in": w(ks[15], (L, D_MODEL, 2 * FFN_HIDDEN), D_MODEL),
        "w_ffn_out": w(ks[16], (L, FFN_HIDDEN, D_MODEL), FFN_HIDDEN),
        "norm_ffn_post": gain(ks[17], (L, D_MODEL)),
    }


def reference(x, norm_mix_pre, w_in, w_alpha_up, b_alpha, gla_norm, w_gla_out,
              conv_w, conv_b, conv_ln_g, conv_ln_b, w_conv_out, w_out, norm_mix_post,
              norm_ffn_pre, w_ffn_in, w_ffn_out, norm_ffn_post):
    split_idx = [int(i) for i in np.cumsum(IN_SIZES)[:-1]]
    for l in range(DEPTH):
        h = rms_norm(x, norm_mix_pre[l])
        proj = h @ w_in[l]
        q, k, v, r, a_lr, u_glu, g_gla, g_conv = jnp.split(proj, split_idx, axis=-1)
        y_gla = gla_branch(q, k, v, r, a_lr, w_alpha_up[l], b_alpha[l], gla_norm[l], w_gla_out[l])
        y_conv = conv_branch(u_glu, conv_w[l], conv_b[l], conv_ln_g[l], conv_ln_b[l], w_conv_out[l])
        merged = jax.nn.sigmoid(g_gla) * y_gla + jax.nn.sigmoid(g_conv) * y_conv
        x = x + rms_norm(merged @ w_out[l], norm_mix_post[l])
        h = rms_norm(x, norm_ffn_pre[l])
        gate, up = jnp.split(h @ w_ffn_in[l], 2, axis=-1)
        f = (jax.nn.silu(gate) * up) @ w_ffn_out[l]
        x = x + rms_norm(f, norm_ffn_post[l])
    return x
```

```python
import numpy as np
import concourse.bass as bass
import concourse.mybir as mybir
from concourse.bass_utils import run_bass_kernel_spmd

F32 = mybir.dt.float32
BF16 = mybir.dt.bfloat16
AF = mybir.ActivationFunctionType
ALU = mybir.AluOpType

D = 1024
S = 2048
NT = S // 128
KT = D // 128
H = 4
DK = 128
DV = 256
FH = 2816
FKT = FH // 128
INC = 7184
CK = 31
EPS = 1e-6

ENGS = ("pe", "act", "dve", "pool", "sp")


class Op:
    __slots__ = ("eng", "idx", "fn", "deps", "dma", "dma_ord", "need_inc", "inc_val")

    def __init__(self, eng, idx, fn, deps, dma):
        self.eng, self.idx, self.fn, self.deps, self.dma = eng, idx, fn, deps, dma
        self.dma_ord = 0
        self.need_inc = False
        self.inc_val = 0


class _Rec:
    def __init__(self):
        self.calls = []

    def __getattr__(self, name):
        def f(*a, **kw):
            self.calls.append((name, a, kw))
            return self
        return f


class Sched:
    def __init__(self):
        self.ops = {e: [] for e in ENGS}
        self.last_w = {}
        self.readers = {}
        self.dma_count = {}
        self.pending = {}
        self.ver = {}

    def fence(self, skip_dma_prefix=("slot", "wfo")):
        snap = {}
        for e in ENGS:
            for o in reversed(self.ops[e]):
                if o.dma is None:
                    snap[("eng", e)] = o.idx
                    break
        for k, c in self.dma_count.items():
            if k.startswith(tuple(skip_dma_prefix)):
                continue
            snap[("dma", k)] = c
        for e in ENGS:
            cur = self.pending.setdefault(e, {})
            for k, v in snap.items():
                cur[k] = max(cur.get(k, -1), v)

    def op(self, eng, fn, r=(), w=(), dma=None, vr=None, vw=None):
        if vr:
            for k, v in vr.items():
                assert self.ver.get(k) == v, ("version mismatch on read", k, self.ver.get(k), v)
        if vw:
            for k, v in vw.items():
                self.ver[k] = v
        rec = _Rec()
        fn(rec)
        assert len(rec.calls) == 1, rec.calls
        _name, _a, _kw = rec.calls[0]
        fn = lambda e, _name=_name, _a=_a, _kw=_kw: getattr(e, _name)(*_a, **_kw)
        idx = len(self.ops[eng])
        deps = {}
        def add(d, raw):
            e, i, o = d
            if o.dma is not None:
                k = ("dma", o.dma)
                deps[k] = max(deps.get(k, 0), self.dma_count[o.dma])
            elif e != eng or (raw and eng != "pe"):
                k = ("eng", e)
                deps[k] = max(deps.get(k, -1), i)
        for k in r:
            if k in self.last_w:
                add(self.last_w[k], True)
        for k in w:
            if k in self.last_w:
                add(self.last_w[k], False)
            for rd in self.readers.get(k, {}).values():
                add(rd, False)
        pf = self.pending.pop(eng, None)
        if pf:
            for k, v in pf.items():
                if k == ("eng", eng):
                    continue
                deps[k] = max(deps.get(k, -1), v)
        o = Op(eng, idx, fn, deps, dma)
        if dma is not None:
            self.dma_count[dma] = self.dma_count.get(dma, 0) + 1
            o.dma_ord = self.dma_count[dma]
        self.ops[eng].append(o)
        me = (eng, idx, o)
        for k in w:
            self.last_w[k] = me
            self.readers[k] = {}
        rk = ("dma", dma, idx) if dma is not None else eng
        for k in r:
            self.readers.setdefault(k, {})[rk] = me
        return o

    def emit(self, nc, sems, dma_sems, block):
        for e in ENGS:
            for o in self.ops[e]:
                for k, v in o.deps.items():
                    if k[0] == "eng":
                        self.ops[k[1]][v].need_inc = True
        for e in ENGS:
            c = 0
            for o in self.ops[e]:
                if o.dma is None and o.need_inc:
                    c += 1
                o.inc_val = c
        handles = {"pe": block.tensor, "act": block.scalar, "dve": block.vector,
                   "pool": block.gpsimd, "sp": block.sync}

        def make(e):
            def body(eng):
                waited = {}
                for o in self.ops[e]:
                    for k, v in o.deps.items():
                        if k[0] == "eng":
                            sem = sems[k[1]]
                            val = self.ops[k[1]][v].inc_val
                        else:
                            sem = dma_sems[k[1]]
                            val = 16 * v
                        if waited.get(k, 0) >= val:
                            continue
                        waited[k] = val
                        eng.wait_ge(sem, val)
                    ins = o.fn(eng)
                    if o.dma is not None:
                        ins.then_inc(dma_sems[o.dma], 16)
                    elif o.need_inc:
                        ins.then_inc(sems[e], 1)
                if e in ("sp", "pool"):
                    fin = {}
                    for o in self.ops[e]:
                        if o.dma is not None:
                            fin[o.dma] = max(fin.get(o.dma, 0), o.dma_ord)
                    for k, v in fin.items():
                        if waited.get(("dma", k), 0) < 16 * v:
                            eng.wait_ge(dma_sems[k], 16 * v)
            return body

        for e in ENGS:
            if self.ops[e]:
                handles[e](make(e))


class Arena:
    def __init__(self, total_bytes):
        self.total = total_bytes
        self.off = 0
        self.marks = []
        self.peak = 0

    def alloc(self, nbytes, align=64):
        o = (self.off + align - 1) // align * align
        self.off = o + nbytes
        self.peak = max(self.peak, self.off)
        assert self.off <= self.total, ("SBUF arena overflow", self.off, self.total)
        return o

    def push(self):
        self.marks.append(self.off)

    def pop(self):
        self.off = self.marks.pop()


def pipeline_steps(n, stages):
    maxs = max(sk for sk, _ in stages)
    order = sorted(stages, key=lambda x: -x[0])
    steps = []
    for step in range(n + maxs):
        def run(step=step):
            for sk, fn in order:
                t = step - sk
                if 0 <= t < n:
                    fn(t)
        steps.append(run)
    return steps


def pipeline(n, stages):
    maxs = max(sk for sk, _ in stages)
    for step in range(n + maxs):
        for sk, fn in sorted(stages, key=lambda x: -x[0]):
            t = step - sk
            if 0 <= t < n:
                fn(t)


STAGES = ("P0", "GLA", "CONVIN", "CONV", "MERGE", "WOUT", "FFN1", "FFN2")
C_A = 3088
C_B = 4112
C_GA = 5136
C_GB = 6160


def build_nc(debug=None, stop_after=None):
    nc = bass.Bass("TRN2", target_bir_lowering=False)
    x_d = nc.dram_tensor("x", [S, D], F32, kind="ExternalInput").ap()
    w_in_d = nc.dram_tensor("w_in", [D, INC], F32, kind="ExternalInput").ap()
    w_go_d = nc.dram_tensor("w_gla_out", [D, D], F32, kind="ExternalInput").ap()
    w_co_d = nc.dram_tensor("w_conv_out", [D, D], F32, kind="ExternalInput").ap()
    w_o_d = nc.dram_tensor("w_out", [D, D], F32, kind="ExternalInput").ap()
    w_f1_d = nc.dram_tensor("w_ffn_in", [D, 2 * FH], F32, kind="ExternalInput").ap()
    w_f2_d = nc.dram_tensor("w_ffn_out", [FH, D], F32, kind="ExternalInput").ap()
    bc_d = nc.dram_tensor("bc", [128, 5 * D], F32, kind="ExternalInput").ap()
    cpar_d = nc.dram_tensor("cpar", [128, KT * 34], F32, kind="ExternalInput").ap()
    wup_d = nc.dram_tensor("wup", [32, 512], F32, kind="ExternalInput").ap()
    cst_d = nc.dram_tensor("cst", [128, 128], F32, kind="ExternalInput").ap()
    out_d = nc.dram_tensor("out", [S, D], F32, kind="ExternalOutput").ap()
    x1_d = nc.dram_tensor("x1scr", [S, D], F32, kind="Internal").ap()
    dbg_d = None
    if debug is not None:
        dbg_d = nc.dram_tensor("dbg", list(debug["shape"]), F32, kind="ExternalOutput").ap()

    ARENA_BYTES = 212736
    ar = Arena(ARENA_BYTES)
    sc = Sched()
    last_stage = stop_after or STAGES[-1]
    enabled = STAGES[:STAGES.index(last_stage) + 1]

    import contextlib
    with contextlib.ExitStack() as es:
        arena = es.enter_context(nc.sbuf_tensor("arena", [128, ARENA_BYTES // 2], BF16))
        psum = es.enter_context(nc.psum_tensor("psum", [128, 8, 512], F32))

        def sb(nbytes, dtype, pattern=None, **kw):
            off = ar.alloc(nbytes)
            a = arena[:, off // 2:(off + nbytes) // 2]
            if dtype == F32:
                a = a.bitcast(F32)
            if pattern:
                a = a.rearrange(pattern, **kw)
            return a

        def ps(bank, dtype=F32):
            a = psum[:, bank, :]
            if dtype == BF16:
                a = a.bitcast(BF16)
            return a

        dump_off = [0]

        def dump(ap2d, rkeys, n):
            ar.push()
            dtmp = sb(2048 * 4, F32)
            for c0 in range(0, n, 2048):
                c1 = min(n, c0 + 2048)
                sc.op("dve", lambda e, c0=c0, c1=c1: e.tensor_copy(out=dtmp[:, 0:c1 - c0], in_=ap2d[:, c0:c1]),
                      r=list(rkeys), w=["dtmp"])
                o = dump_off[0]
                sc.op("sp", lambda e, c0=c0, c1=c1, o=o: e.dma_start(out=dbg_d[:, o:o + c1 - c0], in_=dtmp[:, 0:c1 - c0]),
                      r=["dtmp"], dma="dbg")
                dump_off[0] += c1 - c0
            ar.pop()

        hT = sb(KT * S * 2, BF16, "p (k t) -> p k t", k=KT)
        bcg = sb(2 * D * 4, F32, "p (g d) -> p g d", g=2)
        ident = sb(128 * 2, BF16)
        ident_f = sb(128 * 4, F32)
        tri = sb(128 * 4, F32)
        triU = sb(128 * 4, F32)
        ones_b = sb(128 * 2, BF16)
        cpar = sb(KT * 34 * 4, F32, "p (k j) -> p k j", k=KT)
        wbig = sb(FKT * D * 2, BF16, "p (k c) -> p k c", k=FKT)
        slots = [wbig[:, 0:8, :], wbig[:, 8:16, :]]
        base_mark = ar.off

        HKEYS = [("h", "T", t) for t in range(NT)]

        def hkeys(tb):
            return HKEYS[tb * 4:(tb + 1) * 4]

        bc_state = {}

        def load_gain(gi, buf):
            sc.op("sp", lambda e: e.dma_start(out=bcg[:, buf, :], in_=bc_d[:, gi * D:(gi + 1) * D]),
                  w=[("bcg", buf)], dma=f"bcgh{buf}")

        sc.op("pool", lambda e: e.dma_start(out=bcg[:, 0, :], in_=bc_d[:, 0:D]), w=[("bcg", 0)], dma="bcg0")
        sc.op("pool", lambda e: e.dma_start(out=tri, in_=cst_d), w=["tri"], dma="tri")
        sc.op("pool", lambda e: e.dma_start(out=cpar.rearrange("p k j -> p (k j)"), in_=cpar_d), w=["cpar"], dma="cpar")
        sc.op("pool", lambda e: e.memset(ident_f, 0.0), w=["ident_f"])
        sc.op("pool", lambda e: e.affine_select(out=ident_f, in_=ident_f, pattern=[[-1, 128]], base=0,
                                                channel_multiplier=1, compare_op=ALU.not_equal, fill=1.0),
              r=["ident_f"], w=["ident_f"])
        sc.op("dve", lambda e: e.tensor_copy(out=ident, in_=ident_f), r=["ident_f"], w=["ident"])
        sc.op("dve", lambda e: e.memset(ones_b, 1.0), w=["ones_b"])
        sc.op("dve", lambda e: e.tensor_scalar(out=triU, in0=tri, scalar1=-1.0 / 16.0, scalar2=None, op0=ALU.mult),
              r=["tri"], w=["triU"])

        slot_gen = [0, 0]

        def wload(slot, parts, key=None):
            k = key or ("slot", slot)
            for (src, c0) in parts:
                n = src.shape[1]
                nk = src.shape[0] // 128
                dst = slots[slot][:, 0:nk, c0:c0 + n]
                sc.op("pool", lambda e, src=src, dst=dst: e.dma_start(out=dst, in_=src.rearrange("(k p) c -> p k c", p=128)),
                      w=[k], dma=f"slot{slot}")

        def make_norm(gbuf, dstT, tag, tkey, srcs, copy_eng="mix"):
            junk = sb(D * 2, BF16)
            hb = [sb(D * 2, BF16) for _ in range(2)]
            ss = sb(NT * 4, F32)
            lnv = sb(NT * 4, F32)
            rstd = sb(NT * 4, F32)

            def N1(t):
                src, skey, sv = srcs(t)
                sc.op("act", lambda e: e.activation(out=junk, in_=src, func=AF.Square, accum_out=ss[:, t:t + 1]),
                      r=[skey], w=[tag + "junk", (tag, "ss", t)], vr=sv)
                sc.op("act", lambda e: e.activation(out=lnv[:, t:t + 1], in_=ss[:, t:t + 1], func=AF.Ln, scale=1.0 / D, bias=EPS),
                      r=[(tag, "ss", t)], w=[(tag, "ln", t)])
                sc.op("act", lambda e: e.activation(out=rstd[:, t:t + 1], in_=lnv[:, t:t + 1], func=AF.Exp, scale=-0.5),
                      r=[(tag, "ln", t)], w=[(tag, "rstd", t)])

            def N2(t):
                src, skey, sv = srcs(t)
                b = t % 2
                sc.op("dve", lambda e: e.scalar_tensor_tensor(out=hb[b], in0=src, scalar=rstd[:, t:t + 1], in1=bcg[:, gbuf, :],
                                                              op0=ALU.mult, op1=ALU.mult),
                      r=[skey, (tag, "rstd", t), ("bcg", gbuf)], w=[(tag, "hb", b)], vr=sv, vw={(tag, "hb", b): t})

            def N3(t):
                b = t % 2
                pb = 6 + b
                pst = ps(pb, BF16).rearrange("p (k t) -> p k t", k=KT)
                for k in range(KT):
                    sc.op("pe", lambda e, k=k: e.transpose(out=pst[:, k, :], in_=hb[b][:, k * 128:(k + 1) * 128], identity=ident),
                          r=[(tag, "hb", b), "ident"], w=[("ps", pb)], vr={(tag, "hb", b): t}, vw={("ps", pb): (tag, t)})

            def N4(t):
                b = t % 2
                pb = 6 + b
                pst = ps(pb, BF16).rearrange("p (k t) -> p k t", k=KT)
                if (t % 2 == 0 and copy_eng == "mix") or copy_eng == "act":
                    sc.op("act", lambda e: e.activation(out=dstT[:, :, t * 128:(t + 1) * 128], in_=pst, func=AF.Copy),
                          r=[("ps", pb)], w=[tkey(t)], vr={("ps", pb): (tag, t)})
                else:
                    sc.op("dve", lambda e: e.tensor_copy(out=dstT[:, :, t * 128:(t + 1) * 128], in_=pst),
                          r=[("ps", pb)], w=[tkey(t)], vr={("ps", pb): (tag, t)})
            return N1, N2, N3, N4

        ar.push()
        EARLY_HOLE = 52 * 1024
        ar.alloc(EARLY_HOLE)
        xs = [sb(2 * D * 4, F32, "p (n d) -> p n d", n=2) for _ in range(4)]

        def P0_load(t):
            if t % 2:
                return
            b = (t // 2) % 4
            sc.op("sp", lambda e: e.dma_start(out=xs[b], in_=x_d[t * 128:(t + 2) * 128, :].rearrange("(n p) d -> p n d", p=128)),
                  w=[("xs", b)], dma=f"xs{b}", vw={("xs", b): t // 2})

        N1, N2, N3, N4 = make_norm(0, hT, "h", lambda t: ("h", "T", t),
                                   lambda t: (xs[(t // 2) % 4][:, t % 2, :], ("xs", (t // 2) % 4), {("xs", (t // 2) % 4): t // 2}),
                                   copy_eng="dve")
        pipeline(NT, [(0, P0_load), (2, N1), (3, N2), (4, N3), (5, N4)])
        ar.pop()
        if debug is not None and debug["what"] == "hT":
            dump(hT.rearrange("p k t -> p (k t)"), HKEYS, KT * S)

        mmrot = [0]

        mmbanks = [[0, 1, 2]]

        def mmbank():
            lst = mmbanks[0]
            b = lst[mmrot[0] % len(lst)]
            mmrot[0] += 1
            return b

        def mm_fm(out_ps, bank, lhs_of_k, lkeys, rhs_of_k, rkeys, nk=KT):
            for k in range(nk):
                l_ap = lhs_of_k(k)
                r_ap = rhs_of_k(k)
                sc.op("pe", lambda e, k=k, l_ap=l_ap, r_ap=r_ap: e.matmul(out_ps, l_ap, r_ap, start=(k == 0), stop=(k == nk - 1)),
                      r=list(lkeys) + list(rkeys), w=[("ps", bank)])

        def convin_fill(i):
            wload(i % 2, [(w_in_d[:, C_A + i * 512:C_A + (i + 1) * 512], 0),
                          (w_in_d[:, C_B + i * 512:C_B + (i + 1) * 512], 512)])

        def merge_fill(i):
            wload(i % 2, [(w_go_d[:, i * 256:(i + 1) * 256], 0),
                          (w_in_d[:, C_GA + i * 256:C_GA + (i + 1) * 256], 256),
                          (w_co_d[:, i * 256:(i + 1) * 256], 512),
                          (w_in_d[:, C_GB + i * 256:C_GB + (i + 1) * 256], 768)])

        def ffn1_fill(i):
            nt_ = min(4, FKT - 4 * i)
            wload((i + 1) % 2, [(w_f1_d[:, 4 * i * 128:(4 * i + nt_) * 128], 0),
                                (w_f1_d[:, FH + 4 * i * 128:FH + (4 * i + nt_) * 128], 512)])

        og = sb(KT * S * 2, BF16, "p (k t) -> p k t", k=KT)
        def gla_fill(h):
            wload(h % 2, [(w_in_d[:, h * 128:(h + 1) * 128], 0),
                          (w_in_d[:, 512 + h * 128:512 + (h + 1) * 128], 128),
                          (w_in_d[:, 1024 + h * 256:1024 + (h + 1) * 256], 256),
                          (w_in_d[:, 2048 + h * 256:2048 + (h + 1) * 256], 512)])

        if "GLA" in enabled:
            gla_fill(0)
            load_gain(4, 1)
            ar.push()
            alrT = sb(S * 4, F32)
            wup = sb(512 * 4, F32)
            walr = sb(KT * 16 * 2, BF16, "p (k c) -> p k c", k=KT)
            lns_t = sb(4, F32)
            vv = sb(NT * DV * 2, BF16, "p (n e) -> p n e", n=NT)
            assert ar.off - base_mark <= EARLY_HOLE, (ar.off - base_mark, EARLY_HOLE)
            qdT = sb(S * 2, BF16)
            kdT = sb(S * 2, BF16)
            kteT = sb(S * 2, BF16)
            kte = sb(S * 2, BF16, "p (n d) -> p n d", n=NT)
            EqT = sb(S * 4, F32)
            EkT = sb(S * 4, F32)
            EteT = sb(S * 4, F32)
            lsp = sb(S * 4, F32)
            Llast = sb(NT * 4, F32)
            nLlast = sb(NT * 4, F32)
            ELast = sb(NT * 4, F32)
            Sf2 = [sb(DV * 4, F32) for _ in range(2)]
            Sb = [sb(DV * 2, BF16) for _ in range(2)]
            smAll = sb(NT * 128 * 2, BF16, "p (n i) -> p n i", n=NT)
            onb = [sb(DV * 2, BF16) for _ in range(2)]
            ojunk = sb(DV * 2, BF16)
            oss = sb(NT * 4, F32)
            oln = sb(NT * 4, F32)
            orstd = sb(NT * 4, F32)
            tri_bc = tri.unsqueeze(1).broadcast_to([128, 4, 128])

            sc.op("sp", lambda e: e.dma_start(out=wup[0:32, :], in_=wup_d), w=["wup"], dma="wup")
            sc.op("pool", lambda e: e.memset(alrT[0:32, :], 1.0), w=["alrT"])
            sc.op("pool", lambda e: e.dma_start(out=walr, in_=w_in_d[:, 3072:3088].rearrange("(k p) c -> p k c", p=128)),
                  w=["walr"], dma="walr")
            for tb in range(4):
                b = mmbank()
                mm_fm(ps(b)[0:16, :], b, lambda k: walr[:, k, :], ["walr"],
                      lambda k, tb=tb: hT[:, k, tb * 512:(tb + 1) * 512], hkeys(tb))
                sc.op("dve", lambda e, b=b, tb=tb: e.tensor_copy(out=alrT[0:16, tb * 512:(tb + 1) * 512], in_=ps(b)[0:16, :]),
                      r=[("ps", b)], w=["alrT"])
                for n2 in (2 * tb, 2 * tb + 1):
                    b = mmbank()
                    for c in range(2):
                        n = n2 * 2 + c
                        mm_fm(ps(b)[:, c * 256:(c + 1) * 256], b, lambda k: hT[:, k, n * 128:(n + 1) * 128], [("h", "T", n)],
                              lambda k: slots[0][:, k, 256:512], [("slot", 0)])
                    vdst = vv[:, n2 * 2:n2 * 2 + 2, :].rearrange("p n e -> p (n e)")
                    if n2 % 2 == 0:
                        sc.op("act", lambda e: e.activation(out=vdst, in_=ps(b), func=AF.Copy), r=[("ps", b)], w=[("v", n2)])
                    else:
                        sc.op("dve", lambda e: e.tensor_copy(out=vdst, in_=ps(b)), r=[("ps", b)], w=[("v", n2)])

            LNS = float(np.log(DK ** -0.5))
            sc.op("dve", lambda e: e.memset(lns_t, LNS), w=["lns"])
            def psO(s):
                return ps(3 + s)[:, 0:256]

            def psU(s):
                return ps((5, 2)[s])[:, 0:256]

            def psT(s):
                return ps(6 + s, BF16)[:, 0:256].rearrange("p (j t) -> p j t", j=2)

            def kO(s_):
                return ("ps", 3 + s_)

            def kU(s_):
                return ("ps", (5, 2)[s_])

            def kT(s_):
                return ("ps", 6 + s_)

            mmbanks[0] = [0, 1, 2, 3, 4, 5]
            for h in range(H):
                slot = h % 2
                if h > 0:
                    gla_fill(h)
                sk = ("slot", slot)
                W = slots[slot]
                if h == H - 1 and "CONVIN" in enabled:
                    convin_fill(0)
                for g in range(4):
                    b = mmbank()
                    for c in range(4):
                        n = g * 4 + c
                        sc.op("pe", lambda e: e.matmul(ps(b)[:, c * 128:(c + 1) * 128], alrT[0:32, n * 128:(n + 1) * 128],
                                                       wup[0:32, h * 128:(h + 1) * 128], start=True, stop=True),
                              r=["alrT", "wup"], w=[("ps", b)])
                    sc.op("act", lambda e: e.activation(out=lsp[:, g * 512:(g + 1) * 512], in_=ps(b), func=AF.Exp, scale=-1.0),
                          r=[("ps", b)], w=[("lsp", g)])
                    sc.op("act", lambda e: e.activation(out=lsp[:, g * 512:(g + 1) * 512], in_=lsp[:, g * 512:(g + 1) * 512],
                                                        func=AF.Ln, bias=1.0),
                          r=[("lsp", g)], w=[("lsp", g)])
                for n2 in (range(NT // 2) if h > 0 else ()):
                    b = mmbank()
                    for c in range(2):
                        n = n2 * 2 + c
                        mm_fm(ps(b)[:, c * 256:(c + 1) * 256], b, lambda k: hT[:, k, n * 128:(n + 1) * 128], [("h", "T", n)],
                              lambda k: W[:, k, 256:512], [sk])
                    sc.op("dve", lambda e: e.tensor_copy(out=vv[:, n2 * 2:n2 * 2 + 2, :].rearrange("p n e -> p (n e)"), in_=ps(b)),
                          r=[("ps", b)], w=[("v", n2)])
                for g in range(4):
                    b = mmbank()
                    for c in range(4):
                        n = g * 4 + c
                        sc.op("pe", lambda e: e.matmul(ps(b)[:, c * 128:(c + 1) * 128], lsp[:, n * 128:(n + 1) * 128],
                                                       triU, start=True, stop=True),
                              r=[("lsp", g), "triU"], w=[("ps", b)])
                    sc.op("dve", lambda e: e.tensor_copy(out=Llast[:, g * 4:(g + 1) * 4],
                                                         in_=ps(b).rearrange("p (c t) -> p c t", c=4)[:, :, 127]),
                          r=[("ps", b)], w=[("Llast", g)])
                    sc.op("act", lambda e: e.activation(out=EqT[:, g * 512:(g + 1) * 512], in_=ps(b), func=AF.Exp, bias=lns_t[:, 0:1]),
                          r=[("ps", b), "lns"], w=[("Eq", g)])
                    sc.op("act", lambda e: e.activation(out=EkT[:, g * 512:(g + 1) * 512], in_=ps(b), func=AF.Exp, scale=-1.0),
                          r=[("ps", b)], w=[("Ek", g)])
                    for c in range(4):
                        n = g * 4 + c
                        sc.op("act", lambda e: e.activation(out=EteT[:, n * 128:(n + 1) * 128], in_=ps(b)[:, c * 128:(c + 1) * 128],
                                                            func=AF.Exp, scale=-1.0, bias=Llast[:, n:n + 1]),
                              r=[("ps", b), ("Llast", g)], w=[("Ete", g)])
                    sc.op("act", lambda e: e.activation(out=ELast[:, g * 4:(g + 1) * 4], in_=Llast[:, g * 4:(g + 1) * 4], func=AF.Exp),
                          r=[("Llast", g)], w=[("ELast", g)])
                for tb in range(4):
                    tsl = slice(tb * 512, (tb + 1) * 512)
                    b = mmbank()
                    mm_fm(ps(b), b, lambda k: W[:, k, 0:128], [sk], lambda k: hT[:, k, tsl], hkeys(tb))
                    sc.op("dve", lambda e: e.tensor_tensor(out=qdT[:, tsl], in0=ps(b), in1=EqT[:, tsl], op=ALU.mult),
                          r=[("ps", b), ("Eq", tb)], w=[("qdT", tb)])
                    b = mmbank()
                    mm_fm(ps(b), b, lambda k: W[:, k, 128:256], [sk], lambda k: hT[:, k, tsl], hkeys(tb))
                    sc.op("dve", lambda e: e.tensor_tensor(out=kdT[:, tsl], in0=ps(b), in1=EkT[:, tsl], op=ALU.mult),
                          r=[("ps", b), ("Ek", tb)], w=[("kdT", tb)])
                    sc.op("dve", lambda e: e.tensor_tensor(out=kteT[:, tsl], in0=ps(b), in1=EteT[:, tsl], op=ALU.mult),
                          r=[("ps", b), ("Ete", tb)], w=[("kteT", tb)])
                for half in range(2):
                    b = mmbank()
                    pst = ps(b, BF16).rearrange("p (n d) -> p n d", n=8)
                    for c in range(8):
                        n = half * 8 + c
                        sc.op("pe", lambda e: e.transpose(out=pst[:, c, :], in_=kteT[:, n * 128:(n + 1) * 128], identity=ident),
                              r=[("kteT", n // 4), "ident"], w=[("ps", b)])
                    sc.op("dve", lambda e: e.tensor_copy(out=kte[:, half * 8:(half + 1) * 8, :], in_=pst),
                          r=[("ps", b)], w=[("kte", half)])
                for g in range(4):
                    b = mmbank()
                    for c in range(4):
                        n = g * 4 + c
                        cs = slice(n * 128, (n + 1) * 128)
                        sc.op("pe", lambda e: e.matmul(ps(b)[:, c * 128:(c + 1) * 128], kdT[:, cs], qdT[:, cs], start=True, stop=True),
                              r=[("kdT", g), ("qdT", g)], w=[("ps", b)])
                    sc.op("dve", lambda e: e.tensor_tensor(out=smAll[:, g * 4:(g + 1) * 4, :], in0=ps(b).rearrange("p (c i) -> p c i", c=4),
                                                           in1=tri_bc, op=ALU.mult),
                          r=[("ps", b), "tri"], w=[("sm", g)])
                for j in range(2):
                    for tb in range(4):
                        tsl = slice(tb * 512, (tb + 1) * 512)
                        b = mmbank()
                        mm_fm(ps(b), b, lambda k: W[:, k, 512 + j * 128:512 + (j + 1) * 128], [sk], lambda k: hT[:, k, tsl], hkeys(tb))
                        sc.op("act", lambda e: e.activation(out=og[:, 2 * h + j, tsl], in_=ps(b), func=AF.Silu),
                              r=[("ps", b)], w=[("og", h, tb)])

                def Umm(n):
                    sc.op("pe", lambda e: e.matmul(psU(n % 2), kte[:, n, :], vv[:, n, :], start=True, stop=True),
                          r=[("kte", n // 8), ("v", n // 2)], w=[kU(n % 2)], vw={("psU", n % 2): (h, n)})

                def R0(n):
                    if n == 0:
                        Umm(0)
                    if n + 1 < NT - 1:
                        Umm(n + 1)
                    s = n % 2
                    cs = slice(n * 128, (n + 1) * 128)
                    sc.op("pe", lambda e: e.matmul(psO(s), smAll[:, n, :], vv[:, n, :], start=True, stop=(n == 0)),
                          r=[("sm", n // 4), ("v", n // 2)], w=[kO(s)], vw={("psO", s): (h, n)})
                    if n > 0:
                        sc.op("pe", lambda e: e.matmul(psO(s), qdT[:, cs], Sb[(n - 1) % 2], start=False, stop=True),
                              r=[("qdT", n // 4), ("Sb", (n - 1) % 2)], w=[kO(s)], vr={("Sb", (n - 1) % 2): (h, n - 1)})

                def R1(n):
                    s = n % 2
                    if n < NT - 1:
                        sb_ = n % 2
                        if n == 0:
                            sc.op("dve", lambda e: e.tensor_copy(out=Sb[sb_], in_=psU(0)), r=[kU(0)], w=[("Sb", sb_)],
                                  vr={("psU", 0): (h, 0)}, vw={("Sb", sb_): (h, 0)})
                            sc.op("dve", lambda e: e.tensor_copy(out=Sf2[0], in_=psU(0)), r=[kU(0)], w=[("Sf", 0)])
                        else:
                            sc.op("dve", lambda e: e.scalar_tensor_tensor(out=Sb[sb_], in0=Sf2[(n - 1) % 2], scalar=ELast[:, n:n + 1], in1=psU(n % 2),
                                                                          op0=ALU.mult, op1=ALU.add),
                                  r=[kU(n % 2), ("Sf", (n - 1) % 2), ("ELast", n // 4)], w=[("Sb", sb_)],
                                  vr={("psU", n % 2): (h, n)}, vw={("Sb", sb_): (h, n)})
                            sc.op("dve", lambda e: e.scalar_tensor_tensor(out=Sf2[n % 2], in0=Sf2[(n - 1) % 2], scalar=ELast[:, n:n + 1], in1=psU(n % 2),
                                                                          op0=ALU.mult, op1=ALU.add),
                                  r=[kU(n % 2), ("Sf", (n - 1) % 2), ("ELast", n // 4)], w=[("Sf", n % 2)])
                    sc.op("act", lambda e: e.activation(out=ojunk, in_=psO(s), func=AF.Square, accum_out=oss[:, n:n + 1]),
                          r=[kO(s)], w=["ojunk", ("oss", n)], vr={("psO", s): (h, n)})
                    sc.op("act", lambda e: e.activation(out=oln[:, n:n + 1], in_=oss[:, n:n + 1], func=AF.Ln, scale=1.0 / DV, bias=EPS),
                          r=[("oss", n)], w=[("oln", n)])
                    sc.op("act", lambda e: e.activation(out=orstd[:, n:n + 1], in_=oln[:, n:n + 1], func=AF.Exp, scale=-0.5),
                          r=[("oln", n)], w=[("orstd", n)])

                def R2(n):
                    s = n % 2
                    sc.op("dve", lambda e: e.scalar_tensor_tensor(out=onb[n % 2], in0=psO(s), scalar=orstd[:, n:n + 1],
                                                                  in1=bcg[:, 1, h * 256:(h + 1) * 256], op0=ALU.mult, op1=ALU.mult),
                          r=[kO(s), ("orstd", n), ("bcg", 1)], w=[("onb", n % 2)], vr={("psO", s): (h, n)}, vw={("onb", n % 2): (h, n)})

                def R3(n):
                    s = n % 2
                    for j in range(2):
                        sc.op("pe", lambda e, j=j: e.transpose(out=psT(s)[:, j, :], in_=onb[n % 2][:, j * 128:(j + 1) * 128], identity=ident),
                              r=[("onb", n % 2), "ident"], w=[kT(s)], vr={("onb", n % 2): (h, n)}, vw={("psT", s): (h, n)})

                def R4(n):
                    s = n % 2
                    osl = og[:, 2 * h:2 * h + 2, n * 128:(n + 1) * 128]
                    sc.op("dve", lambda e: e.tensor_tensor(out=osl, in0=psT(s), in1=osl, op=ALU.mult),
                          r=[kT(s), ("og", h, n // 4)], w=[("og", h, n // 4)], vr={("psT", s): (h, n)})

                pipeline(NT, [(0, R0), (1, R1), (2, R2), (3, R3), (4, R4)])
            ar.pop()
        OGKEYS = [("og", h, tb) for h in range(H) for tb in range(4)]
        if debug is not None and debug["what"] == "og":
            dump(og.rearrange("p k t -> p (k t)"), OGKEYS, KT * S)

        ZW = S + 32
        ar.push()
        zT = sb(KT * ZW * 2, BF16, "p (k t) -> p k t", k=KT)
        cv = sb(KT * S * 2, BF16, "p (k t) -> p k t", k=KT)
        CVKEYS = [("cv", c, tb) for c in range(KT) for tb in range(4)]
        ar.push()
        diag = [sb(CK * 128 * 2, BF16, "p (j c) -> p j c", j=CK) for _ in range(2)]
        sgc = [sb(512 * 4, F32) for _ in range(2)]
        if "CONVIN" in enabled:
            if "GLA" not in enabled:
                convin_fill(0)
            convin_fill(1)
            sc.fence()
            mmbanks[0] = [0, 1, 2, 3, 4, 5]
            ar.push()
            sg = sgc
            sc.op("pool", lambda e: e.memset(zT[:, :, 0:32], 0.0), w=[("zT", c) for c in range(KT)])
            cnt = 0
            for i in range(2):
                slot = i % 2
                sk = ("slot", slot)
                W = slots[slot]
                for c in range(4):
                    ct = 4 * i + c
                    for tb in range(4):
                        bA = mmbank()
                        mm_fm(ps(bA), bA, lambda k: W[:, k, c * 128:(c + 1) * 128], [sk],
                              lambda k: hT[:, k, tb * 512:(tb + 1) * 512], hkeys(tb))
                        bB = mmbank()
                        mm_fm(ps(bB), bB, lambda k: W[:, k, 512 + c * 128:512 + (c + 1) * 128], [sk],
                              lambda k: hT[:, k, tb * 512:(tb + 1) * 512], hkeys(tb))
                        j = cnt % 2
                        cnt += 1
                        sc.op("act", lambda e, bB=bB, j=j: e.activation(out=sg[j], in_=ps(bB), func=AF.Sigmoid),
                              r=[("ps", bB)], w=[("sg", j)])
                        sc.op("dve", lambda e, bA=bA, j=j, ct=ct, tb=tb: e.tensor_tensor(
                            out=zT[:, ct, 32 + tb * 512:32 + (tb + 1) * 512], in0=ps(bA), in1=sg[j], op=ALU.mult),
                            r=[("ps", bA), ("sg", j)], w=[("zT", ct)])
            ar.pop()
        if debug is not None and debug["what"] == "zT":
            for c in range(KT):
                dump(zT[:, c, 32:32 + S], [("zT", c)], S)

        if "CONV" in enabled:
            ar.push()
            KD = 6
            f8 = [wbig[:, 16 + i, :].bitcast(F32) for i in range(6)] + [bcg[:, 0, 0:512], bcg[:, 0, 512:1024]]
            sq = [bcg[:, 1, i * 256:(i + 1) * 256].bitcast(BF16) for i in range(4)]
            msq, var, t1 = f8[0:2], f8[2:4], f8[4:8]

            def L0(tb):
                ts_ = slice(tb * 512, (tb + 1) * 512)
                p = tb % 2
                for c in range(KT):
                    j = c % 4
                    if j % 2 == 0:
                        sc.op("pool", lambda e: e.tensor_tensor(out=sq[j], in0=cv[:, c, ts_], in1=cv[:, c, ts_], op=ALU.mult),
                              r=[("cv", c, tb)], w=[("sq", j)])
                    else:
                        sc.op("act", lambda e: e.activation(out=sq[j], in_=cv[:, c, ts_], func=AF.Square),
                              r=[("cv", c, tb)], w=[("sq", j)])
                    sc.op("pe", lambda e: e.matmul(ps(2 + p), ones_b, cv[:, c, ts_], start=(c == 0), stop=(c == KT - 1)),
                          r=["ones_b", ("cv", c, tb)], w=[("ps", 2 + p)])
                    sc.op("pe", lambda e: e.matmul(ps(4 + p), ones_b, sq[j], start=(c == 0), stop=(c == KT - 1)),
                          r=["ones_b", ("sq", j)], w=[("ps", 4 + p)])

            def L1(tb):
                p = tb % 2
                sc.op("act", lambda e: e.activation(out=msq[p], in_=ps(2 + p), func=AF.Square, scale=1.0 / D),
                      r=[("ps", 2 + p)], w=[("f8", p)])
                sc.op("dve", lambda e: e.scalar_tensor_tensor(out=var[p], in0=ps(4 + p), scalar=1.0 / D, in1=msq[p],
                                                              op0=ALU.mult, op1=ALU.subtract),
                      r=[("ps", 4 + p), ("f8", p)], w=[("f8", 2 + p)])
                sc.op("act", lambda e: e.activation(out=var[p], in_=var[p], func=AF.Ln, bias=EPS), r=[("f8", 2 + p)], w=[("f8", 2 + p)])
                sc.op("act", lambda e: e.activation(out=ps(6 + p), in_=var[p], func=AF.Exp, scale=-0.5),
                      r=[("f8", 2 + p)], w=[("ps", 6 + p)])

            def L2(tb):
                ts_ = slice(tb * 512, (tb + 1) * 512)
                p = tb % 2
                for c2 in range(KT // 2):
                    for c in (2 * c2, 2 * c2 + 1):
                        j = c % 4
                        sc.op("dve", lambda e: e.scalar_tensor_tensor(out=t1[j], in0=ps(2 + p), scalar=-1.0 / D, in1=cv[:, c, ts_],
                                                                      op0=ALU.mult, op1=ALU.add),
                              r=[("ps", 2 + p), ("cv", c, tb)], w=[("f8", 4 + j)])
                    for c in (2 * c2, 2 * c2 + 1):
                        j = c % 4
                        sc.op("dve", lambda e: e.tensor_tensor(out=t1[j], in0=ps(6 + p), in1=t1[j], op=ALU.mult),
                              r=[("f8", 4 + j), ("ps", 6 + p)], w=[("f8", 4 + j)])
                    for c in (2 * c2, 2 * c2 + 1):
                        j = c % 4
                        sc.op("act", lambda e: e.activation(out=cv[:, c, ts_], in_=t1[j], func=AF.Silu,
                                                            scale=cpar[:, c, 32:33], bias=cpar[:, c, 33:34]),
                              r=[("f8", 4 + j), "cpar"], w=[("cv", c, tb)])

            ln_steps = pipeline_steps(4, [(0, L0), (1, L1), (2, L2)])

            for c in range(KT):
                db = c % 2
                last = (c == KT - 1)
                KDc = 0 if last else KD
                KPc = CK - KDc
                mmbanks[0] = [0, 1] if last else [0, 1, 2, 3]
                for j in range(KPc):
                    sc.op("pool", lambda e, c=c, j=j, db=db: e.tensor_scalar(out=diag[db][:, j, :], in0=ident_f, scalar1=cpar[:, c, j:j + 1],
                                                                             scalar2=0.0, op0=ALU.mult, op1=ALU.add),
                          r=["ident_f", "cpar"], w=[("diag", db)])
                for tb in range(4):
                    b = mmbank()
                    for j in range(KPc):
                        l_ap = diag[db][:, j, :]
                        r_ap = zT[:, c, 2 + j + tb * 512:2 + j + (tb + 1) * 512]
                        sc.op("pe", lambda e, b=b, j=j, l_ap=l_ap, r_ap=r_ap, KPc=KPc: e.matmul(ps(b), l_ap, r_ap, start=(j == 0), stop=(j == KPc - 1)),
                              r=[("diag", db), ("zT", c)], w=[("ps", b)])
                    for j in range(KPc, CK):
                        z_ap = zT[:, c, 2 + j + tb * 512:2 + j + (tb + 1) * 512]
                        sc.op("dve", lambda e, z_ap=z_ap, j=j, b=b: e.scalar_tensor_tensor(out=ps(b), in0=z_ap, scalar=cpar[:, c, j:j + 1], in1=ps(b),
                                                                                        op0=ALU.mult, op1=ALU.add),
                              r=[("ps", b), ("zT", c), "cpar"], w=[("ps", b)])
                    sc.op("act", lambda e, c=c, tb=tb, b=b: e.activation(out=cv[:, c, tb * 512:(tb + 1) * 512], in_=ps(b), func=AF.Identity,
                                                                         bias=cpar[:, c, 31:32]),
                          r=[("ps", b), "cpar"], w=[("cv", c, tb)])
                    if last and tb >= 1:
                        ln_steps[tb - 1]()
            for st in ln_steps[3:]:
                st()
            ar.pop()
            if debug is not None and debug["what"] == "cv":
                dump(cv.rearrange("p k t -> p (k t)"), CVKEYS, KT * S)
            if "MERGE" in enabled:
                merge_fill(0)
                merge_fill(1)
            ar.pop()
            ar.push()
        if "CONV" not in enabled:
            ar.pop()
            ar.push()
            f8 = [sb(512 * 4, F32) for _ in range(8)]
        if debug is not None and debug["what"] == "zc":
            dump(cv.rearrange("p k t -> p (k t)"), CVKEYS, KT * S)

        mT = zT
        MKEYS = [("mT", k, tb) for k in range(KT) for tb in range(4)]
        if "MERGE" in enabled:
            mmbanks[0] = [0, 1, 2, 3, 4, 5]
            sgA, sgB, m1, m2 = f8[0:2], f8[2:4], f8[4:6], f8[6:8]
            cnt = 0
            for i in range(4):
                slot = i % 2
                if i >= 2:
                    merge_fill(i)
                sk = ("slot", slot)
                W = slots[slot]
                for c in range(2):
                    ct = 2 * i + c
                    for tb in range(4):
                        tsl = slice(tb * 512, (tb + 1) * 512)
                        j = cnt % 2
                        cnt += 1
                        bYA = mmbank()
                        mm_fm(ps(bYA), bYA, lambda k: W[:, k, c * 128:(c + 1) * 128], [sk],
                              lambda k: og[:, k, tsl], [("og", hh, tb) for hh in range(H)])
                        bGA = mmbank()
                        mm_fm(ps(bGA), bGA, lambda k: W[:, k, 256 + c * 128:256 + (c + 1) * 128], [sk],
                              lambda k: hT[:, k, tsl], hkeys(tb))
                        sc.op("act", lambda e, bGA=bGA, j=j: e.activation(out=sgA[j], in_=ps(bGA), func=AF.Sigmoid),
                              r=[("ps", bGA)], w=[("f8", j)])
                        sc.op("dve", lambda e, bYA=bYA, j=j: e.tensor_tensor(out=m1[j], in0=ps(bYA), in1=sgA[j], op=ALU.mult),
                              r=[("ps", bYA), ("f8", j)], w=[("f8", 4 + j)])
                        bYB = mmbank()
                        mm_fm(ps(bYB), bYB, lambda k: W[:, k, 512 + c * 128:512 + (c + 1) * 128], [sk],
                              lambda k: cv[:, k, tsl], [("cv", kk, tb) for kk in range(KT)])
                        bGB = mmbank()
                        mm_fm(ps(bGB), bGB, lambda k: W[:, k, 768 + c * 128:768 + (c + 1) * 128], [sk],
                              lambda k: hT[:, k, tsl], hkeys(tb))
                        sc.op("act", lambda e, bGB=bGB, j=j: e.activation(out=sgB[j], in_=ps(bGB), func=AF.Sigmoid),
                              r=[("ps", bGB)], w=[("f8", 2 + j)])
                        sc.op("dve", lambda e, bYB=bYB, j=j: e.tensor_tensor(out=m2[j], in0=ps(bYB), in1=sgB[j], op=ALU.mult),
                              r=[("ps", bYB), ("f8", 2 + j)], w=[("f8", 6 + j)])
                        sc.op("dve", lambda e, j=j, ct=ct, tsl=tsl: e.tensor_tensor(out=mT[:, ct, tsl], in0=m1[j], in1=m2[j], op=ALU.add),
                              r=[("f8", 4 + j), ("f8", 6 + j)], w=[("mT", ct, tb), ("zT", ct)])
            ar.pop()
        if debug is not None and debug["what"] == "mT":
            for c in range(KT):
                dump(mT[:, c, 0:S], [("mT", c, tb) for tb in range(4)], S)

        def make_epilogue(gbuf, tag, resid_of, out_of, ntmp=2):
            junk = sb(D * 2, BF16)
            tmp = [sb(D * 4, F32) for _ in range(ntmp)]
            ss = sb(NT * 4, F32)
            lnv = sb(NT * 4, F32)
            rstd = sb(NT * 4, F32)

            def pair(t):
                p = t % 3
                return psum[:, 2 * p:2 * p + 2, :].rearrange("p b n -> p (b n)"), [("ps", 2 * p), ("ps", 2 * p + 1)]

            def E1(t):
                pa, pk = pair(t)
                sc.op("act", lambda e: e.activation(out=junk, in_=pa, func=AF.Square, accum_out=ss[:, t:t + 1]),
                      r=pk, w=[tag + "junk", (tag, "ss", t)], vr={pk[0]: (tag, t)})
                sc.op("act", lambda e: e.activation(out=lnv[:, t:t + 1], in_=ss[:, t:t + 1], func=AF.Ln, scale=1.0 / D, bias=EPS),
                      r=[(tag, "ss", t)], w=[(tag, "lnv", t)])
                sc.op("act", lambda e: e.activation(out=rstd[:, t:t + 1], in_=lnv[:, t:t + 1], func=AF.Exp, scale=-0.5),
                      r=[(tag, "lnv", t)], w=[(tag, "rstd", t)])

            def E2(t):
                pa, pk = pair(t)
                res, rkey, rv = resid_of(t)
                outb, okey, ov = out_of(t)
                tb_ = t % ntmp
                sc.op("dve", lambda e: e.scalar_tensor_tensor(out=pa, in0=pa, scalar=rstd[:, t:t + 1], in1=bcg[:, gbuf, :],
                                                              op0=ALU.mult, op1=ALU.mult),
                      r=pk + [(tag, "rstd", t), ("bcg", gbuf)], w=pk, vr={pk[0]: (tag, t)})
                sc.op("dve", lambda e: e.tensor_tensor(out=outb, in0=pa, in1=res, op=ALU.add),
                      r=pk + [rkey], w=[okey], vr=rv, vw=ov)
            return E1, E2

        if "WOUT" in enabled:
            wload(0, [(w_o_d, 0)])
            if "FFN1" in enabled:
                ffn1_fill(0)
            sc.fence()
            ar.pop()
            ar.push()
            zT_keep = sb(KT * ZW * 2, BF16)
            load_gain(1, 0)
            load_gain(2, 1)
            xs = [sb(D * 4, F32) for _ in range(3)]
            x1s = [sb(D * 4, F32) for _ in range(3)]
            E1, E2 = make_epilogue(0, "wo", lambda t: (xs[t % 3], ("xs", t % 3), {("xs", t % 3): t}),
                                   lambda t: (x1s[t % 3], ("x1s", t % 3), {("x1s", t % 3): t}))
            N1, N2, N3, N4 = make_norm(1, hT, "h2", lambda t: ("h", "T", t),
                                       lambda t: (x1s[t % 3], ("x1s", t % 3), {("x1s", t % 3): t}))
            Wo = slots[0]

            def W0(t):
                p = t % 3
                for hf in range(2):
                    bk = 2 * p + hf
                    for k in range(KT):
                        sc.op("pe", lambda e, k=k: e.matmul(ps(bk), mT[:, k, t * 128:(t + 1) * 128], Wo[:, k, hf * 512:(hf + 1) * 512],
                                                            start=(k == 0), stop=(k == KT - 1)),
                              r=[("mT", k, t // 4), ("slot", 0)], w=[("ps", bk)], vw={("ps", bk): ("wo", t)})
                sc.op("sp", lambda e: e.dma_start(out=xs[t % 3], in_=x_d[t * 128:(t + 1) * 128, :]),
                      w=[("xs", t % 3)], dma=f"xs{t % 3}", vw={("xs", t % 3): t})

            def W3(t):
                sc.op("sp", lambda e: e.dma_start(out=x1_d[t * 128:(t + 1) * 128, :], in_=x1s[t % 3]),
                      r=[("x1s", t % 3)], w=[("x1d", t)], dma=f"x1st{t % 3}", vr={("x1s", t % 3): t})
                N1(t)

            pipeline(NT, [(0, W0), (1, E1), (2, E2), (3, W3), (4, N2), (5, N3), (6, N4)])
            if debug is not None and debug["what"] == "h2T":
                dump(hT.rearrange("p k t -> p (k t)"), HKEYS, KT * S)
            ar.pop()
        else:
            ar.pop()

        if "FFN1" in enabled:
            sc.fence()
            ar.off = base_mark
            aT = sb(FKT * S * 2, BF16, "p (k t) -> p k t", k=FKT)
            if "FFN2" in enabled:
                sc.op("pool", lambda e: e.dma_start(out=wbig[:, 16:FKT, :],
                                                    in_=w_f2_d[16 * 128:FKT * 128, :].rearrange("(k p) c -> p k c", p=128)),
                      w=[("wfo", 2)], dma="wfo2")
            AKEYS = lambda tb: [("aT", k, tb) for k in range(FKT)]
            sg = [sb(512 * 4, F32) for _ in range(2)]
            cnt = 0
            nfill = (FKT + 3) // 4
            for i in range(nfill):
                nt_ = min(4, FKT - 4 * i)
                slot = (i + 1) % 2
                if i >= 1:
                    ffn1_fill(i)
                sk = ("slot", slot)
                W = slots[slot]
                for c in range(nt_):
                    ct = 4 * i + c
                    for tb in range(4):
                        tsl = slice(tb * 512, (tb + 1) * 512)
                        j = cnt % 2
                        cnt += 1
                        bG = mmbank()
                        mm_fm(ps(bG), bG, lambda k: W[:, k, c * 128:(c + 1) * 128], [sk], lambda k: hT[:, k, tsl], hkeys(tb))
                        bU = mmbank()
                        mm_fm(ps(bU), bU, lambda k: W[:, k, 512 + c * 128:512 + (c + 1) * 128], [sk], lambda k: hT[:, k, tsl], hkeys(tb))
                        sc.op("act", lambda e, bG=bG, j=j: e.activation(out=sg[j], in_=ps(bG), func=AF.Silu),
                              r=[("ps", bG)], w=[("sg", j)])
                        sc.op("dve", lambda e, bU=bU, j=j, ct=ct, tsl=tsl: e.tensor_tensor(out=aT[:, ct, tsl], in0=ps(bU), in1=sg[j], op=ALU.mult),
                              r=[("ps", bU), ("sg", j)], w=[("aT", ct, tb)])
            if debug is not None and debug["what"] == "aT":
                for k in range(FKT):
                    dump(aT[:, k, :], [("aT", k, tb) for tb in range(4)], S)

        if "FFN2" in enabled:
            ar.push()
            sc.op("pool", lambda e: e.dma_start(out=wbig[:, 8:16, :], in_=w_f2_d[8 * 128:16 * 128, :].rearrange("(k p) c -> p k c", p=128)),
                  w=[("wfo", 1), ("slot", 1)], dma="wfo1")
            sc.op("pool", lambda e: e.dma_start(out=wbig[:, 0:8, :], in_=w_f2_d[0:8 * 128, :].rearrange("(k p) c -> p k c", p=128)),
                  w=[("wfo", 0), ("slot", 0)], dma="wfo0")
            load_gain(3, 0)
            xs2 = [sb(D * 4, F32) for _ in range(2)]
            osb = [sb(D * 4, F32) for _ in range(2)]
            E1, E2 = make_epilogue(0, "fo", lambda t: (xs2[t % 2], ("xs2", t % 2), {("xs2", t % 2): t}),
                                   lambda t: (osb[t % 2], ("osb", t % 2), {("osb", t % 2): t}), ntmp=2)

            korder = list(range(16, FKT)) + list(range(8, 16)) + list(range(8))

            def F0_part(t, k_lo, k_hi):
                p = t % 3
                for hf in range(2):
                    bk = 2 * p + hf
                    for ki in range(k_lo, k_hi):
                        k = korder[ki]
                        sc.op("pe", lambda e, k=k, ki=ki: e.matmul(ps(bk), aT[:, k, t * 128:(t + 1) * 128], wbig[:, k, hf * 512:(hf + 1) * 512],
                                                                   start=(ki == 0), stop=(ki == FKT - 1)),
                              r=[("aT", k, t // 4), ("wfo", k // 8)], w=[("ps", bk)], vw={("ps", bk): ("fo", t)})

            for t0 in range(3):
                F0_part(t0, 0, FKT - 8)

            def F0(t):
                if t < 3:
                    F0_part(t, FKT - 8, FKT)
                else:
                    F0_part(t, 0, FKT)
                sc.op("sp", lambda e: e.dma_start(out=xs2[t % 2], in_=x1_d[t * 128:(t + 1) * 128, :]),
                      r=[("x1d", t)], w=[("xs2", t % 2)], dma=f"xs2_{t % 2}", vw={("xs2", t % 2): t})

            def F3(t):
                sc.op("sp", lambda e: e.dma_start(out=out_d[t * 128:(t + 1) * 128, :], in_=osb[t % 2]),
                      r=[("osb", t % 2)], dma=f"ost{t % 2}", vr={("osb", t % 2): t})

            pipeline(NT, [(0, F0), (1, E1), (2, E2), (3, F3)])
            ar.pop()

        def finish():
            dma_keys = sorted(sc.dma_count.keys())
            sem_objs = {}
            for e in ENGS:
                sem_objs[e] = es.enter_context(nc.semaphore("s_" + e))
            dsem = {}
            for k in dma_keys:
                dsem[k] = es.enter_context(nc.semaphore("d_" + k))
            block = es.enter_context(nc.Block())
            sc.emit(nc, sem_objs, dsem, block)

        finish()
    return nc


def host_prep(inputs):
    g = np.stack([inputs["norm_mix_pre"][0], inputs["norm_mix_post"][0], inputs["norm_ffn_pre"][0],
                  inputs["norm_ffn_post"][0], inputs["gla_norm"][0]], axis=0).reshape(1, 5 * D)
    bc = np.ascontiguousarray(np.broadcast_to(g, (128, 5 * D))).astype(np.float32)
    cp = np.zeros((128, KT, 34), np.float32)
    cp[:, :, 0:31] = inputs["conv_w"][0].T.reshape(KT, 128, CK).transpose(1, 0, 2)
    cp[:, :, 31] = inputs["conv_b"][0].reshape(KT, 128).T
    cp[:, :, 32] = inputs["conv_ln_g"][0].reshape(KT, 128).T
    cp[:, :, 33] = inputs["conv_ln_b"][0].reshape(KT, 128).T
    wup = np.zeros((32, 512), np.float32)
    wup[0:16] = inputs["w_alpha_up"][0]
    wup[16] = inputs["b_alpha"][0]
    cst = np.triu(np.ones((128, 128), np.float32))
    shared = {"w_in": np.ascontiguousarray(inputs["w_in"][0]),
              "w_gla_out": np.ascontiguousarray(inputs["w_gla_out"][0]),
              "w_conv_out": np.ascontiguousarray(inputs["w_conv_out"][0]),
              "w_out": np.ascontiguousarray(inputs["w_out"][0]),
              "w_ffn_in": np.ascontiguousarray(inputs["w_ffn_in"][0]),
              "w_ffn_out": np.ascontiguousarray(inputs["w_ffn_out"][0]),
              "bc": bc, "cpar": np.ascontiguousarray(cp.reshape(128, KT * 34)), "wup": wup, "cst": cst}
    return shared


def kernel(_debug=None, _stop=None, **inputs):
    inputs = {k: np.asarray(v) for k, v in inputs.items()}
    shared = host_prep(inputs)
    nc = build_nc(_debug, _stop)
    in_maps = []
    for c in range(8):
        m = dict(shared)
        m["x"] = np.ascontiguousarray(inputs["x"][c])
        in_maps.append(m)
    res = run_bass_kernel_spmd(nc, in_maps, core_ids=list(range(8)))
    if _debug is not None:
        return [r["dbg"] for r in res.results]
    return np.stack([r["out"] for r in res.results], axis=0).astype(np.float32)
```

```python
import numpy as np
import concourse.bass as bass
import concourse.mybir as mybir
from concourse.bass_utils import run_bass_kernel_spmd

F32 = mybir.dt.float32
BF16 = mybir.dt.bfloat16
AF = mybir.ActivationFunctionType
ALU = mybir.AluOpType

D = 1024
S = 2048
NT = S // 128
KT = D // 128
H = 4
DK = 128
DV = 256
FH = 2816
FKT = FH // 128
INC = 7184
CK = 31
EPS = 1e-6

ENGS = ("pe", "act", "dve", "pool", "sp")


class Op:
    __slots__ = ("eng", "idx", "fn", "deps", "dma", "dma_ord", "need_inc", "inc_val")

    def __init__(self, eng, idx, fn, deps, dma):
        self.eng, self.idx, self.fn, self.deps, self.dma = eng, idx, fn, deps, dma
        self.dma_ord = 0
        self.need_inc = False
        self.inc_val = 0


class _Rec:
    def __init__(self):
        self.calls = []

    def __getattr__(self, name):
        def f(*a, **kw):
            self.calls.append((name, a, kw))
            return self
        return f


class Sched:
    def __init__(self):
        self.ops = {e: [] for e in ENGS}
        self.last_w = {}
        self.readers = {}
        self.dma_count = {}
        self.pending = {}
        self.ver = {}

    def fence(self, skip_dma_prefix=("slot", "wfo")):
        snap = {}
        for e in ENGS:
            for o in reversed(self.ops[e]):
                if o.dma is None:
                    snap[("eng", e)] = o.idx
                    break
        for k, c in self.dma_count.items():
            if k.startswith(tuple(skip_dma_prefix)):
                continue
            snap[("dma", k)] = c
        for e in ENGS:
            cur = self.pending.setdefault(e, {})
            for k, v in snap.items():
                cur[k] = max(cur.get(k, -1), v)

    def op(self, eng, fn, r=(), w=(), dma=None, vr=None, vw=None):
        if vr:
            for k, v in vr.items():
                assert self.ver.get(k) == v, ("version mismatch on read", k, self.ver.get(k), v)
        if vw:
            for k, v in vw.items():
                self.ver[k] = v
        rec = _Rec()
        fn(rec)
        assert len(rec.calls) == 1, rec.calls
        _name, _a, _kw = rec.calls[0]
        fn = lambda e, _name=_name, _a=_a, _kw=_kw: getattr(e, _name)(*_a, **_kw)
        idx = len(self.ops[eng])
        deps = {}
        def add(d, raw):
            e, i, o = d
            if o.dma is not None:
                k = ("dma", o.dma)
                deps[k] = max(deps.get(k, 0), self.dma_count[o.dma])
            elif e != eng or (raw and eng != "pe"):
                k = ("eng", e)
                deps[k] = max(deps.get(k, -1), i)
        for k in r:
            if k in self.last_w:
                add(self.last_w[k], True)
        for k in w:
            if k in self.last_w:
                add(self.last_w[k], False)
            for rd in self.readers.get(k, {}).values():
                add(rd, False)
        pf = self.pending.pop(eng, None)
        if pf:
            for k, v in pf.items():
                if k == ("eng", eng):
                    continue
                deps[k] = max(deps.get(k, -1), v)
        o = Op(eng, idx, fn, deps, dma)
        if dma is not None:
            self.dma_count[dma] = self.dma_count.get(dma, 0) + 1
            o.dma_ord = self.dma_count[dma]
        self.ops[eng].append(o)
        me = (eng, idx, o)
        for k in w:
            self.last_w[k] = me
            self.readers[k] = {}
        rk = ("dma", dma, idx) if dma is not None else eng
        for k in r:
            self.readers.setdefault(k, {})[rk] = me
        return o

    def emit(self, nc, sems, dma_sems, block):
        for e in ENGS:
            for o in self.ops[e]:
                for k, v in o.deps.items():
                    if k[0] == "eng":
                        self.ops[k[1]][v].need_inc = True
        for e in ENGS:
            c = 0
            for o in self.ops[e]:
                if o.dma is None and o.need_inc:
                    c += 1
                o.inc_val = c
        handles = {"pe": block.tensor, "act": block.scalar, "dve": block.vector,
                   "pool": block.gpsimd, "sp": block.sync}

        def make(e):
            def body(eng):
                waited = {}
                for o in self.ops[e]:
                    for k, v in o.deps.items():
                        if k[0] == "eng":
                            sem = sems[k[1]]
                            val = self.ops[k[1]][v].inc_val
                        else:
                            sem = dma_sems[k[1]]
                            val = 16 * v
                        if waited.get(k, 0) >= val:
                            continue
                        waited[k] = val
                        eng.wait_ge(sem, val)
                    ins = o.fn(eng)
                    if o.dma is not None:
                        ins.then_inc(dma_sems[o.dma], 16)
                    elif o.need_inc:
                        ins.then_inc(sems[e], 1)
                if e in ("sp", "pool"):
                    fin = {}
                    for o in self.ops[e]:
                        if o.dma is not None:
                            fin[o.dma] = max(fin.get(o.dma, 0), o.dma_ord)
                    for k, v in fin.items():
                        if waited.get(("dma", k), 0) < 16 * v:
                            eng.wait_ge(dma_sems[k], 16 * v)
            return body

        for e in ENGS:
            if self.ops[e]:
                handles[e](make(e))


class Arena:
    def __init__(self, total_bytes):
        self.total = total_bytes
        self.off = 0
        self.marks = []
        self.peak = 0

    def alloc(self, nbytes, align=64):
        o = (self.off + align - 1) // align * align
        self.off = o + nbytes
        self.peak = max(self.peak, self.off)
        assert self.off <= self.total, ("SBUF arena overflow", self.off, self.total)
        return o

    def push(self):
        self.marks.append(self.off)

    def pop(self):
        self.off = self.marks.pop()


def pipeline_steps(n, stages):
    maxs = max(sk for sk, _ in stages)
    order = sorted(stages, key=lambda x: -x[0])
    steps = []
    for step in range(n + maxs):
        def run(step=step):
            for sk, fn in order:
                t = step - sk
                if 0 <= t < n:
                    fn(t)
        steps.append(run)
    return steps


def pipeline(n, stages):
    maxs = max(sk for sk, _ in stages)
    for step in range(n + maxs):
        for sk, fn in sorted(stages, key=lambda x: -x[0]):
            t = step - sk
            if 0 <= t < n:
                fn(t)


STAGES = ("P0", "GLA", "CONVIN", "CONV", "MERGE", "WOUT", "FFN1", "FFN2")
C_A = 3088
C_B = 4112
C_GA = 5136
C_GB = 6160


def build_nc(debug=None, stop_after=None):
    nc = bass.Bass("TRN2", target_bir_lowering=False)
    x_d = nc.dram_tensor("x", [S, D], F32, kind="ExternalInput").ap()
    w_in_d = nc.dram_tensor("w_in", [D, INC], F32, kind="ExternalInput").ap()
    w_go_d = nc.dram_tensor("w_gla_out", [D, D], F32, kind="ExternalInput").ap()
    w_co_d = nc.dram_tensor("w_conv_out", [D, D], F32, kind="ExternalInput").ap()
    w_o_d = nc.dram_tensor("w_out", [D, D], F32, kind="ExternalInput").ap()
    w_f1_d = nc.dram_tensor("w_ffn_in", [D, 2 * FH], F32, kind="ExternalInput").ap()
    w_f2_d = nc.dram_tensor("w_ffn_out", [FH, D], F32, kind="ExternalInput").ap()
    bc_d = nc.dram_tensor("bc", [128, 5 * D], F32, kind="ExternalInput").ap()
    cpar_d = nc.dram_tensor("cpar", [128, KT * 34], F32, kind="ExternalInput").ap()
    wup_d = nc.dram_tensor("wup", [32, 512], F32, kind="ExternalInput").ap()
    cst_d = nc.dram_tensor("cst", [128, 128], F32, kind="ExternalInput").ap()
    out_d = nc.dram_tensor("out", [S, D], F32, kind="ExternalOutput").ap()
    x1_d = nc.dram_tensor("x1scr", [S, D], F32, kind="Internal").ap()
    dbg_d = None
    if debug is not None:
        dbg_d = nc.dram_tensor("dbg", list(debug["shape"]), F32, kind="ExternalOutput").ap()

    ARENA_BYTES = 212736
    ar = Arena(ARENA_BYTES)
    sc = Sched()
    last_stage = stop_after or STAGES[-1]
    enabled = STAGES[:STAGES.index(last_stage) + 1]

    import contextlib
    with contextlib.ExitStack() as es:
        arena = es.enter_context(nc.sbuf_tensor("arena", [128, ARENA_BYTES // 2], BF16))
        psum = es.enter_context(nc.psum_tensor("psum", [128, 8, 512], F32))

        def sb(nbytes, dtype, pattern=None, **kw):
            off = ar.alloc(nbytes)
            a = arena[:, off // 2:(off + nbytes) // 2]
            if dtype == F32:
                a = a.bitcast(F32)
            if pattern:
                a = a.rearrange(pattern, **kw)
            return a

        def ps(bank, dtype=F32):
            a = psum[:, bank, :]
            if dtype == BF16:
                a = a.bitcast(BF16)
            return a

        dump_off = [0]

        def dump(ap2d, rkeys, n):
            ar.push()
            dtmp = sb(2048 * 4, F32)
            for c0 in range(0, n, 2048):
                c1 = min(n, c0 + 2048)
                sc.op("dve", lambda e, c0=c0, c1=c1: e.tensor_copy(out=dtmp[:, 0:c1 - c0], in_=ap2d[:, c0:c1]),
                      r=list(rkeys), w=["dtmp"])
                o = dump_off[0]
                sc.op("sp", lambda e, c0=c0, c1=c1, o=o: e.dma_start(out=dbg_d[:, o:o + c1 - c0], in_=dtmp[:, 0:c1 - c0]),
                      r=["dtmp"], dma="dbg")
                dump_off[0] += c1 - c0
            ar.pop()

        hT = sb(KT * S * 2, BF16, "p (k t) -> p k t", k=KT)
        bcg = sb(2 * D * 4, F32, "p (g d) -> p g d", g=2)
        ident = sb(128 * 2, BF16)
        ident_f = sb(128 * 4, F32)
        tri = sb(128 * 4, F32)
        triU = sb(128 * 4, F32)
        ones_b = sb(128 * 2, BF16)
        cpar = sb(KT * 34 * 4, F32, "p (k j) -> p k j", k=KT)
        wbig = sb(FKT * D * 2, BF16, "p (k c) -> p k c", k=FKT)
        slots = [wbig[:, 0:8, :], wbig[:, 8:16, :]]
        base_mark = ar.off

        HKEYS = [("h", "T", t) for t in range(NT)]

        def hkeys(tb):
            return HKEYS[tb * 4:(tb + 1) * 4]

        bc_state = {}

        def load_gain(gi, buf):
            sc.op("sp", lambda e: e.dma_start(out=bcg[:, buf, :], in_=bc_d[:, gi * D:(gi + 1) * D]),
                  w=[("bcg", buf)], dma=f"bcgh{buf}")

        sc.op("pool", lambda e: e.dma_start(out=bcg[:, 0, :], in_=bc_d[:, 0:D]), w=[("bcg", 0)], dma="bcg0")
        sc.op("pool", lambda e: e.dma_start(out=tri, in_=cst_d), w=["tri"], dma="tri")
        sc.op("pool", lambda e: e.dma_start(out=cpar.rearrange("p k j -> p (k j)"), in_=cpar_d), w=["cpar"], dma="cpar")
        sc.op("pool", lambda e: e.memset(ident_f, 0.0), w=["ident_f"])
        sc.op("pool", lambda e: e.affine_select(out=ident_f, in_=ident_f, pattern=[[-1, 128]], base=0,
                                                channel_multiplier=1, compare_op=ALU.not_equal, fill=1.0),
              r=["ident_f"], w=["ident_f"])
        sc.op("dve", lambda e: e.tensor_copy(out=ident, in_=ident_f), r=["ident_f"], w=["ident"])
        sc.op("dve", lambda e: e.memset(ones_b, 1.0), w=["ones_b"])
        sc.op("dve", lambda e: e.tensor_scalar(out=triU, in0=tri, scalar1=-1.0 / 16.0, scalar2=None, op0=ALU.mult),
              r=["tri"], w=["triU"])

        slot_gen = [0, 0]

        def wload(slot, parts, key=None):
            k = key or ("slot", slot)
            for (src, c0) in parts:
                n = src.shape[1]
                nk = src.shape[0] // 128
                dst = slots[slot][:, 0:nk, c0:c0 + n]
                sc.op("pool", lambda e, src=src, dst=dst: e.dma_start(out=dst, in_=src.rearrange("(k p) c -> p k c", p=128)),
                      w=[k], dma=f"slot{slot}")

        def make_norm(gbuf, dstT, tag, tkey, srcs, copy_eng="mix"):
            junk = sb(D * 2, BF16)
            hb = [sb(D * 2, BF16) for _ in range(2)]
            ss = sb(NT * 4, F32)
            lnv = sb(NT * 4, F32)
            rstd = sb(NT * 4, F32)

            def N1(t):
                src, skey, sv = srcs(t)
                sc.op("act", lambda e: e.activation(out=junk, in_=src, func=AF.Square, accum_out=ss[:, t:t + 1]),
                      r=[skey], w=[tag + "junk", (tag, "ss", t)], vr=sv)
                sc.op("act", lambda e: e.activation(out=lnv[:, t:t + 1], in_=ss[:, t:t + 1], func=AF.Ln, scale=1.0 / D, bias=EPS),
                      r=[(tag, "ss", t)], w=[(tag, "ln", t)])
                sc.op("act", lambda e: e.activation(out=rstd[:, t:t + 1], in_=lnv[:, t:t + 1], func=AF.Exp, scale=-0.5),
                      r=[(tag, "ln", t)], w=[(tag, "rstd", t)])

            def N2(t):
                src, skey, sv = srcs(t)
                b = t % 2
                sc.op("dve", lambda e: e.scalar_tensor_tensor(out=hb[b], in0=src, scalar=rstd[:, t:t + 1], in1=bcg[:, gbuf, :],
                                                              op0=ALU.mult, op1=ALU.mult),
                      r=[skey, (tag, "rstd", t), ("bcg", gbuf)], w=[(tag, "hb", b)], vr=sv, vw={(tag, "hb", b): t})

            def N3(t):
                b = t % 2
                pb = 6 + b
                pst = ps(pb, BF16).rearrange("p (k t) -> p k t", k=KT)
                for k in range(KT):
                    sc.op("pe", lambda e, k=k: e.transpose(out=pst[:, k, :], in_=hb[b][:, k * 128:(k + 1) * 128], identity=ident),
                          r=[(tag, "hb", b), "ident"], w=[("ps", pb)], vr={(tag, "hb", b): t}, vw={("ps", pb): (tag, t)})

            def N4(t):
                b = t % 2
                pb = 6 + b
                pst = ps(pb, BF16).rearrange("p (k t) -> p k t", k=KT)
                if (t % 2 == 0 and copy_eng == "mix") or copy_eng == "act":
                    sc.op("act", lambda e: e.activation(out=dstT[:, :, t * 128:(t + 1) * 128], in_=pst, func=AF.Copy),
                          r=[("ps", pb)], w=[tkey(t)], vr={("ps", pb): (tag, t)})
                else:
                    sc.op("dve", lambda e: e.tensor_copy(out=dstT[:, :, t * 128:(t + 1) * 128], in_=pst),
                          r=[("ps", pb)], w=[tkey(t)], vr={("ps", pb): (tag, t)})
            return N1, N2, N3, N4

        ar.push()
        EARLY_HOLE = 52 * 1024
        ar.alloc(EARLY_HOLE)
        xs = [sb(2 * D * 4, F32, "p (n d) -> p n d", n=2) for _ in range(4)]

        def P0_load(t):
            if t % 2:
                return
            b = (t // 2) % 4
            sc.op("sp", lambda e: e.dma_start(out=xs[b], in_=x_d[t * 128:(t + 2) * 128, :].rearrange("(n p) d -> p n d", p=128)),
                  w=[("xs", b)], dma=f"xs{b}", vw={("xs", b): t // 2})

        N1, N2, N3, N4 = make_norm(0, hT, "h", lambda t: ("h", "T", t),
                                   lambda t: (xs[(t // 2) % 4][:, t % 2, :], ("xs", (t // 2) % 4), {("xs", (t // 2) % 4): t // 2}),
                                   copy_eng="dve")
        pipeline(NT, [(0, P0_load), (2, N1), (3, N2), (4, N3), (5, N4)])
        ar.pop()
        if debug is not None and debug["what"] == "hT":
            dump(hT.rearrange("p k t -> p (k t)"), HKEYS, KT * S)

        mmrot = [0]

        mmbanks = [[0, 1, 2]]

        def mmbank():
            lst = mmbanks[0]
            b = lst[mmrot[0] % len(lst)]
            mmrot[0] += 1
            return b

        def mm_fm(out_ps, bank, lhs_of_k, lkeys, rhs_of_k, rkeys, nk=KT):
            for k in range(nk):
                l_ap = lhs_of_k(k)
                r_ap = rhs_of_k(k)
                sc.op("pe", lambda e, k=k, l_ap=l_ap, r_ap=r_ap: e.matmul(out_ps, l_ap, r_ap, start=(k == 0), stop=(k == nk - 1)),
                      r=list(lkeys) + list(rkeys), w=[("ps", bank)])

        def convin_fill(i):
            wload(i % 2, [(w_in_d[:, C_A + i * 512:C_A + (i + 1) * 512], 0),
                          (w_in_d[:, C_B + i * 512:C_B + (i + 1) * 512], 512)])

        def merge_fill(i):
            wload(i % 2, [(w_go_d[:, i * 256:(i + 1) * 256], 0),
                          (w_in_d[:, C_GA + i * 256:C_GA + (i + 1) * 256], 256),
                          (w_co_d[:, i * 256:(i + 1) * 256], 512),
                          (w_in_d[:, C_GB + i * 256:C_GB + (i + 1) * 256], 768)])

        def ffn1_fill(i):
            nt_ = min(4, FKT - 4 * i)
            wload((i + 1) % 2, [(w_f1_d[:, 4 * i * 128:(4 * i + nt_) * 128], 0),
                                (w_f1_d[:, FH + 4 * i * 128:FH + (4 * i + nt_) * 128], 512)])

        og = sb(KT * S * 2, BF16, "p (k t) -> p k t", k=KT)
        def gla_fill(h):
            wload(h % 2, [(w_in_d[:, h * 128:(h + 1) * 128], 0),
                          (w_in_d[:, 512 + h * 128:512 + (h + 1) * 128], 128),
                          (w_in_d[:, 1024 + h * 256:1024 + (h + 1) * 256], 256),
                          (w_in_d[:, 2048 + h * 256:2048 + (h + 1) * 256], 512)])

        if "GLA" in enabled:
            gla_fill(0)
            load_gain(4, 1)
            ar.push()
            alrT = sb(S * 4, F32)
            wup = sb(512 * 4, F32)
            walr = sb(KT * 16 * 2, BF16, "p (k c) -> p k c", k=KT)
            lns_t = sb(4, F32)
            vv = sb(NT * DV * 2, BF16, "p (n e) -> p n e", n=NT)
            assert ar.off - base_mark <= EARLY_HOLE, (ar.off - base_mark, EARLY_HOLE)
            qdT = sb(S * 2, BF16)
            kdT = sb(S * 2, BF16)
            kteT = sb(S * 2, BF16)
            kte = sb(S * 2, BF16, "p (n d) -> p n d", n=NT)
            EqT = sb(S * 4, F32)
            EkT = sb(S * 4, F32)
            EteT = sb(S * 4, F32)
            lsp = sb(S * 4, F32)
            Llast = sb(NT * 4, F32)
            nLlast = sb(NT * 4, F32)
            ELast = sb(NT * 4, F32)
            Sf2 = [sb(DV * 4, F32) for _ in range(2)]
            Sb = [sb(DV * 2, BF16) for _ in range(2)]
            smAll = sb(NT * 128 * 2, BF16, "p (n i) -> p n i", n=NT)
            onb = [sb(DV * 2, BF16) for _ in range(2)]
            ojunk = sb(DV * 2, BF16)
            oss = sb(NT * 4, F32)
            oln = sb(NT * 4, F32)
            orstd = sb(NT * 4, F32)
            tri_bc = tri.unsqueeze(1).broadcast_to([128, 4, 128])

            sc.op("sp", lambda e: e.dma_start(out=wup[0:32, :], in_=wup_d), w=["wup"], dma="wup")
            sc.op("pool", lambda e: e.memset(alrT[0:32, :], 1.0), w=["alrT"])
            sc.op("pool", lambda e: e.dma_start(out=walr, in_=w_in_d[:, 3072:3088].rearrange("(k p) c -> p k c", p=128)),
                  w=["walr"], dma="walr")
            for tb in range(4):
                b = mmbank()
                mm_fm(ps(b)[0:16, :], b, lambda k: walr[:, k, :], ["walr"],
                      lambda k, tb=tb: hT[:, k, tb * 512:(tb + 1) * 512], hkeys(tb))
                sc.op("dve", lambda e, b=b, tb=tb: e.tensor_copy(out=alrT[0:16, tb * 512:(tb + 1) * 512], in_=ps(b)[0:16, :]),
                      r=[("ps", b)], w=["alrT"])
                for n2 in (2 * tb, 2 * tb + 1):
                    b = mmbank()
                    for c in range(2):
                        n = n2 * 2 + c
                        mm_fm(ps(b)[:, c * 256:(c + 1) * 256], b, lambda k: hT[:, k, n * 128:(n + 1) * 128], [("h", "T", n)],
                              lambda k: slots[0][:, k, 256:512], [("slot", 0)])
                    vdst = vv[:, n2 * 2:n2 * 2 + 2, :].rearrange("p n e -> p (n e)")
                    if n2 % 2 == 0:
                        sc.op("act", lambda e: e.activation(out=vdst, in_=ps(b), func=AF.Copy), r=[("ps", b)], w=[("v", n2)])
                    else:
                        sc.op("dve", lambda e: e.tensor_copy(out=vdst, in_=ps(b)), r=[("ps", b)], w=[("v", n2)])

            LNS = float(np.log(DK ** -0.5))
            sc.op("dve", lambda e: e.memset(lns_t, LNS), w=["lns"])
            def psO(s):
                return ps(3 + s)[:, 0:256]

            def psU(s):
                return ps((5, 2)[s])[:, 0:256]

            def psT(s):
                return ps(6 + s, BF16)[:, 0:256].rearrange("p (j t) -> p j t", j=2)

            def kO(s_):
                return ("ps", 3 + s_)

            def kU(s_):
                return ("ps", (5, 2)[s_])

            def kT(s_):
                return ("ps", 6 + s_)

            mmbanks[0] = [0, 1, 2, 3, 4, 5]
            for h in range(H):
                slot = h % 2
                if h > 0:
                    gla_fill(h)
                sk = ("slot", slot)
                W = slots[slot]
                if h == H - 1 and "CONVIN" in enabled:
                    convin_fill(0)
                for g in range(4):
                    b = mmbank()
                    for c in range(4):
                        n = g * 4 + c
                        sc.op("pe", lambda e: e.matmul(ps(b)[:, c * 128:(c + 1) * 128], alrT[0:32, n * 128:(n + 1) * 128],
                                                       wup[0:32, h * 128:(h + 1) * 128], start=True, stop=True),
                              r=["alrT", "wup"], w=[("ps", b)])
                    sc.op("act", lambda e: e.activation(out=lsp[:, g * 512:(g + 1) * 512], in_=ps(b), func=AF.Exp, scale=-1.0),
                          r=[("ps", b)], w=[("lsp", g)])
                    sc.op("act", lambda e: e.activation(out=lsp[:, g * 512:(g + 1) * 512], in_=lsp[:, g * 512:(g + 1) * 512],
                                                        func=AF.Ln, bias=1.0),
                          r=[("lsp", g)], w=[("lsp", g)])
                for n2 in (range(NT // 2) if h > 0 else ()):
                    b = mmbank()
                    for c in range(2):
                        n = n2 * 2 + c
                        mm_fm(ps(b)[:, c * 256:(c + 1) * 256], b, lambda k: hT[:, k, n * 128:(n + 1) * 128], [("h", "T", n)],
                              lambda k: W[:, k, 256:512], [sk])
                    sc.op("dve", lambda e: e.tensor_copy(out=vv[:, n2 * 2:n2 * 2 + 2, :].rearrange("p n e -> p (n e)"), in_=ps(b)),
                          r=[("ps", b)], w=[("v", n2)])
                for g in range(4):
                    b = mmbank()
                    for c in range(4):
                        n = g * 4 + c
                        sc.op("pe", lambda e: e.matmul(ps(b)[:, c * 128:(c + 1) * 128], lsp[:, n * 128:(n + 1) * 128],
                                                       triU, start=True, stop=True),
                              r=[("lsp", g), "triU"], w=[("ps", b)])
                    sc.op("dve", lambda e: e.tensor_copy(out=Llast[:, g * 4:(g + 1) * 4],
                                                         in_=ps(b).rearrange("p (c t) -> p c t", c=4)[:, :, 127]),
                          r=[("ps", b)], w=[("Llast", g)])
                    sc.op("act", lambda e: e.activation(out=EqT[:, g * 512:(g + 1) * 512], in_=ps(b), func=AF.Exp, bias=lns_t[:, 0:1]),
                          r=[("ps", b), "lns"], w=[("Eq", g)])
                    sc.op("act", lambda e: e.activation(out=EkT[:, g * 512:(g + 1) * 512], in_=ps(b), func=AF.Exp, scale=-1.0),
                          r=[("ps", b)], w=[("Ek", g)])
                    for c in range(4):
                        n = g * 4 + c
                        sc.op("act", lambda e: e.activation(out=EteT[:, n * 128:(n + 1) * 128], in_=ps(b)[:, c * 128:(c + 1) * 128],
                                                            func=AF.Exp, scale=-1.0, bias=Llast[:, n:n + 1]),
                              r=[("ps", b), ("Llast", g)], w=[("Ete", g)])
                    sc.op("act", lambda e: e.activation(out=ELast[:, g * 4:(g + 1) * 4], in_=Llast[:, g * 4:(g + 1) * 4], func=AF.Exp),
                          r=[("Llast", g)], w=[("ELast", g)])
                for tb in range(4):
                    tsl = slice(tb * 512, (tb + 1) * 512)
                    b = mmbank()
                    mm_fm(ps(b), b, lambda k: W[:, k, 0:128], [sk], lambda k: hT[:, k, tsl], hkeys(tb))
                    sc.op("dve", lambda e: e.tensor_tensor(out=qdT[:, tsl], in0=ps(b), in1=EqT[:, tsl], op=ALU.mult),
                          r=[("ps", b), ("Eq", tb)], w=[("qdT", tb)])
                    b = mmbank()
                    mm_fm(ps(b), b, lambda k: W[:, k, 128:256], [sk], lambda k: hT[:, k, tsl], hkeys(tb))
                    sc.op("dve", lambda e: e.tensor_tensor(out=kdT[:, tsl], in0=ps(b), in1=EkT[:, tsl], op=ALU.mult),
                          r=[("ps", b), ("Ek", tb)], w=[("kdT", tb)])
                    sc.op("dve", lambda e: e.tensor_tensor(out=kteT[:, tsl], in0=ps(b), in1=EteT[:, tsl], op=ALU.mult),
                          r=[("ps", b), ("Ete", tb)], w=[("kteT", tb)])
                for half in range(2):
                    b = mmbank()
                    pst = ps(b, BF16).rearrange("p (n d) -> p n d", n=8)
                    for c in range(8):
                        n = half * 8 + c
                        sc.op("pe", lambda e: e.transpose(out=pst[:, c, :], in_=kteT[:, n * 128:(n + 1) * 128], identity=ident),
                              r=[("kteT", n // 4), "ident"], w=[("ps", b)])
                    sc.op("dve", lambda e: e.tensor_copy(out=kte[:, half * 8:(half + 1) * 8, :], in_=pst),
                          r=[("ps", b)], w=[("kte", half)])
                for g in range(4):
                    b = mmbank()
                    for c in range(4):
                        n = g * 4 + c
                        cs = slice(n * 128, (n + 1) * 128)
                        sc.op("pe", lambda e: e.matmul(ps(b)[:, c * 128:(c + 1) * 128], kdT[:, cs], qdT[:, cs], start=True, stop=True),
                              r=[("kdT", g), ("qdT", g)], w=[("ps", b)])
                    sc.op("dve", lambda e: e.tensor_tensor(out=smAll[:, g * 4:(g + 1) * 4, :], in0=ps(b).rearrange("p (c i) -> p c i", c=4),
                                                           in1=tri_bc, op=ALU.mult),
                          r=[("ps", b), "tri"], w=[("sm", g)])
                for j in range(2):
                    for tb in range(4):
                        tsl = slice(tb * 512, (tb + 1) * 512)
                        b = mmbank()
                        mm_fm(ps(b), b, lambda k: W[:, k, 512 + j * 128:512 + (j + 1) * 128], [sk], lambda k: hT[:, k, tsl], hkeys(tb))
                        sc.op("act", lambda e: e.activation(out=og[:, 2 * h + j, tsl], in_=ps(b), func=AF.Silu),
                              r=[("ps", b)], w=[("og", h, tb)])

                def Umm(n):
                    sc.op("pe", lambda e: e.matmul(psU(n % 2), kte[:, n, :], vv[:, n, :], start=True, stop=True),
                          r=[("kte", n // 8), ("v", n // 2)], w=[kU(n % 2)], vw={("psU", n % 2): (h, n)})

                def R0(n):
                    if n == 0:
                        Umm(0)
                    if n + 1 < NT - 1:
                        Umm(n + 1)
                    s = n % 2
                    cs = slice(n * 128, (n + 1) * 128)
                    sc.op("pe", lambda e: e.matmul(psO(s), smAll[:, n, :], vv[:, n, :], start=True, stop=(n == 0)),
                          r=[("sm", n // 4), ("v", n // 2)], w=[kO(s)], vw={("psO", s): (h, n)})
                    if n > 0:
                        sc.op("pe", lambda e: e.matmul(psO(s), qdT[:, cs], Sb[(n - 1) % 2], start=False, stop=True),
                              r=[("qdT", n // 4), ("Sb", (n - 1) % 2)], w=[kO(s)], vr={("Sb", (n - 1) % 2): (h, n - 1)})

                def R1(n):
                    s = n % 2
                    if n < NT - 1:
                        sb_ = n % 2
                        if n == 0:
                            sc.op("dve", lambda e: e.tensor_copy(out=Sb[sb_], in_=psU(0)), r=[kU(0)], w=[("Sb", sb_)],
                                  vr={("psU", 0): (h, 0)}, vw={("Sb", sb_): (h, 0)})
                            sc.op("dve", lambda e: e.tensor_copy(out=Sf2[0], in_=psU(0)), r=[kU(0)], w=[("Sf", 0)])
                        else:
                            sc.op("dve", lambda e: e.scalar_tensor_tensor(out=Sb[sb_], in0=Sf2[(n - 1) % 2], scalar=ELast[:, n:n + 1], in1=psU(n % 2),
                                                                          op0=ALU.mult, op1=ALU.add),
                                  r=[kU(n % 2), ("Sf", (n - 1) % 2), ("ELast", n // 4)], w=[("Sb", sb_)],
                                  vr={("psU", n % 2): (h, n)}, vw={("Sb", sb_): (h, n)})
                            sc.op("dve", lambda e: e.scalar_tensor_tensor(out=Sf2[n % 2], in0=Sf2[(n - 1) % 2], scalar=ELast[:, n:n + 1], in1=psU(n % 2),
                                                                          op0=ALU.mult, op1=ALU.add),
                                  r=[kU(n % 2), ("Sf", (n - 1) % 2), ("ELast", n // 4)], w=[("Sf", n % 2)])
                    sc.op("act", lambda e: e.activation(out=ojunk, in_=psO(s), func=AF.Square, accum_out=oss[:, n:n + 1]),
                          r=[kO(s)], w=["ojunk", ("oss", n)], vr={("psO", s): (h, n)})
                    sc.op("act", lambda e: e.activation(out=oln[:, n:n + 1], in_=oss[:, n:n + 1], func=AF.Ln, scale=1.0 / DV, bias=EPS),
                          r=[("oss", n)], w=[("oln", n)])
                    sc.op("act", lambda e: e.activation(out=orstd[:, n:n + 1], in_=oln[:, n:n + 1], func=AF.Exp, scale=-0.5),
                          r=[("oln", n)], w=[("orstd", n)])

                def R2(n):
                    s = n % 2
                    sc.op("dve", lambda e: e.scalar_tensor_tensor(out=onb[n % 2], in0=psO(s), scalar=orstd[:, n:n + 1],
                                                                  in1=bcg[:, 1, h * 256:(h + 1) * 256], op0=ALU.mult, op1=ALU.mult),
                          r=[kO(s), ("orstd", n), ("bcg", 1)], w=[("onb", n % 2)], vr={("psO", s): (h, n)}, vw={("onb", n % 2): (h, n)})

                def R3(n):
                    s = n % 2
                    for j in range(2):
                        sc.op("pe", lambda e, j=j: e.transpose(out=psT(s)[:, j, :], in_=onb[n % 2][:, j * 128:(j + 1) * 128], identity=ident),
                              r=[("onb", n % 2), "ident"], w=[kT(s)], vr={("onb", n % 2): (h, n)}, vw={("psT", s): (h, n)})

                def R4(n):
                    s = n % 2
                    osl = og[:, 2 * h:2 * h + 2, n * 128:(n + 1) * 128]
                    sc.op("dve", lambda e: e.tensor_tensor(out=osl, in0=psT(s), in1=osl, op=ALU.mult),
                          r=[kT(s), ("og", h, n // 4)], w=[("og", h, n // 4)], vr={("psT", s): (h, n)})

                for step in range(NT + 4):
                    for sk_, fn_ in ((1, R1), (4, R4), (3, R3), (2, R2), (0, R0)):
                        t_ = step - sk_
                        if 0 <= t_ < NT:
                            fn_(t_)
            ar.pop()
        OGKEYS = [("og", h, tb) for h in range(H) for tb in range(4)]
        if debug is not None and debug["what"] == "og":
            dump(og.rearrange("p k t -> p (k t)"), OGKEYS, KT * S)

        ZW = S + 32
        ar.push()
        zT = sb(KT * ZW * 2, BF16, "p (k t) -> p k t", k=KT)
        cv = sb(KT * S * 2, BF16, "p (k t) -> p k t", k=KT)
        CVKEYS = [("cv", c, tb) for c in range(KT) for tb in range(4)]
        ar.push()
        diag = [sb(CK * 128 * 2, BF16, "p (j c) -> p j c", j=CK) for _ in range(2)]
        sgc = [sb(512 * 4, F32) for _ in range(2)]
        if "CONVIN" in enabled:
            if "GLA" not in enabled:
                convin_fill(0)
            convin_fill(1)
            sc.fence()
            mmbanks[0] = [0, 1, 2, 3, 4, 5]
            ar.push()
            sg = sgc
            sc.op("pool", lambda e: e.memset(zT[:, :, 0:32], 0.0), w=[("zT", c) for c in range(KT)])
            cnt = 0
            for i in range(2):
                slot = i % 2
                sk = ("slot", slot)
                W = slots[slot]
                for c in range(4):
                    ct = 4 * i + c
                    for tb in range(4):
                        bA = mmbank()
                        mm_fm(ps(bA), bA, lambda k: W[:, k, c * 128:(c + 1) * 128], [sk],
                              lambda k: hT[:, k, tb * 512:(tb + 1) * 512], hkeys(tb))
                        bB = mmbank()
                        mm_fm(ps(bB), bB, lambda k: W[:, k, 512 + c * 128:512 + (c + 1) * 128], [sk],
                              lambda k: hT[:, k, tb * 512:(tb + 1) * 512], hkeys(tb))
                        j = cnt % 2
                        cnt += 1
                        sc.op("act", lambda e, bB=bB, j=j: e.activation(out=sg[j], in_=ps(bB), func=AF.Sigmoid),
                              r=[("ps", bB)], w=[("sg", j)])
                        sc.op("dve", lambda e, bA=bA, j=j, ct=ct, tb=tb: e.tensor_tensor(
                            out=zT[:, ct, 32 + tb * 512:32 + (tb + 1) * 512], in0=ps(bA), in1=sg[j], op=ALU.mult),
                            r=[("ps", bA), ("sg", j)], w=[("zT", ct)])
            ar.pop()
        if debug is not None and debug["what"] == "zT":
            for c in range(KT):
                dump(zT[:, c, 32:32 + S], [("zT", c)], S)

        if "CONV" in enabled:
            ar.push()
            KD = 6
            f8 = [wbig[:, 16 + i, :].bitcast(F32) for i in range(6)] + [bcg[:, 0, 0:512], bcg[:, 0, 512:1024]]
            sq = [bcg[:, 1, i * 256:(i + 1) * 256].bitcast(BF16) for i in range(4)] + \
                 [sgc[i // 2][:, (i % 2) * 256:(i % 2 + 1) * 256].bitcast(BF16) for i in range(4)]
            msq, var, t1 = f8[0:2], f8[2:4], f8[4:8]

            def L0a(tb):
                ts_ = slice(tb * 512, (tb + 1) * 512)
                for c in range(KT):
                    if c % 2 == 0:
                        sc.op("pool", lambda e: e.tensor_tensor(out=sq[c], in0=cv[:, c, ts_], in1=cv[:, c, ts_], op=ALU.mult),
                              r=[("cv", c, tb)], w=[("sq", c)], vw={("sq", c): tb})
                    else:
                        sc.op("act", lambda e: e.activation(out=sq[c], in_=cv[:, c, ts_], func=AF.Square),
                              r=[("cv", c, tb)], w=[("sq", c)], vw={("sq", c): tb})

            def L0b(tb):
                ts_ = slice(tb * 512, (tb + 1) * 512)
                p = tb % 2
                for c in range(KT):
                    sc.op("pe", lambda e: e.matmul(ps(2 + p), ones_b, cv[:, c, ts_], start=(c == 0), stop=(c == KT - 1)),
                          r=["ones_b", ("cv", c, tb)], w=[("ps", 2 + p)])
                    sc.op("pe", lambda e: e.matmul(ps(4 + p), ones_b, sq[c], start=(c == 0), stop=(c == KT - 1)),
                          r=["ones_b", ("sq", c)], w=[("ps", 4 + p)], vr={("sq", c): tb})

            def L1(tb):
                p = tb % 2
                sc.op("act", lambda e: e.activation(out=msq[p], in_=ps(2 + p), func=AF.Square, scale=1.0 / D),
                      r=[("ps", 2 + p)], w=[("f8", p)])
                sc.op("dve", lambda e: e.scalar_tensor_tensor(out=var[p], in0=ps(4 + p), scalar=1.0 / D, in1=msq[p],
                                                              op0=ALU.mult, op1=ALU.subtract),
                      r=[("ps", 4 + p), ("f8", p)], w=[("f8", 2 + p)])
                sc.op("act", lambda e: e.activation(out=var[p], in_=var[p], func=AF.Ln, bias=EPS), r=[("f8", 2 + p)], w=[("f8", 2 + p)])
                sc.op("act", lambda e: e.activation(out=ps(6 + p), in_=var[p], func=AF.Exp, scale=-0.5),
                      r=[("f8", 2 + p)], w=[("ps", 6 + p)])

            def L2(tb):
                ts_ = slice(tb * 512, (tb + 1) * 512)
                p = tb % 2
                for c2 in range(KT // 2):
                    for c in (2 * c2, 2 * c2 + 1):
                        j = c % 4
                        sc.op("dve", lambda e: e.scalar_tensor_tensor(out=t1[j], in0=ps(2 + p), scalar=-1.0 / D, in1=cv[:, c, ts_],
                                                                      op0=ALU.mult, op1=ALU.add),
                              r=[("ps", 2 + p), ("cv", c, tb)], w=[("f8", 4 + j)])
                    for c in (2 * c2, 2 * c2 + 1):
                        j = c % 4
                        sc.op("dve", lambda e: e.tensor_tensor(out=t1[j], in0=ps(6 + p), in1=t1[j], op=ALU.mult),
                              r=[("f8", 4 + j), ("ps", 6 + p)], w=[("f8", 4 + j)])
                    for c in (2 * c2, 2 * c2 + 1):
                        j = c % 4
                        sc.op("act", lambda e: e.activation(out=cv[:, c, ts_], in_=t1[j], func=AF.Silu,
                                                            scale=cpar[:, c, 32:33], bias=cpar[:, c, 33:34]),
                              r=[("f8", 4 + j), "cpar"], w=[("cv", c, tb)])

            def make_ln_step(step):
                def run():
                    if 0 <= step < 4:
                        L0a(step)
                    if 0 <= step - 2 < 4:
                        L2(step - 2)
                    if 0 <= step - 1 < 4:
                        L1(step - 1)
                    if 0 <= step < 4:
                        L0b(step)
                return run

            ln_steps = [make_ln_step(st_) for st_ in range(6)]

            for c in range(KT):
                db = c % 2
                last = (c == KT - 1)
                KDc = 0 if last else KD
                KPc = CK - KDc
                mmbanks[0] = [0, 1] if last else [0, 1, 2, 3]
                for j in range(KPc):
                    sc.op("pool", lambda e, c=c, j=j, db=db: e.tensor_scalar(out=diag[db][:, j, :], in0=ident_f, scalar1=cpar[:, c, j:j + 1],
                                                                             scalar2=0.0, op0=ALU.mult, op1=ALU.add),
                          r=["ident_f", "cpar"], w=[("diag", db)])
                for tb in range(4):
                    b = mmbank()
                    for j in range(KPc):
                        l_ap = diag[db][:, j, :]
                        r_ap = zT[:, c, 2 + j + tb * 512:2 + j + (tb + 1) * 512]
                        sc.op("pe", lambda e, b=b, j=j, l_ap=l_ap, r_ap=r_ap, KPc=KPc: e.matmul(ps(b), l_ap, r_ap, start=(j == 0), stop=(j == KPc - 1)),
                              r=[("diag", db), ("zT", c)], w=[("ps", b)])
                    for j in range(KPc, CK):
                        z_ap = zT[:, c, 2 + j + tb * 512:2 + j + (tb + 1) * 512]
                        sc.op("dve", lambda e, z_ap=z_ap, j=j, b=b: e.scalar_tensor_tensor(out=ps(b), in0=z_ap, scalar=cpar[:, c, j:j + 1], in1=ps(b),
                                                                                        op0=ALU.mult, op1=ALU.add),
                              r=[("ps", b), ("zT", c), "cpar"], w=[("ps", b)])
                    sc.op("act", lambda e, c=c, tb=tb, b=b: e.activation(out=cv[:, c, tb * 512:(tb + 1) * 512], in_=ps(b), func=AF.Identity,
                                                                         bias=cpar[:, c, 31:32]),
                          r=[("ps", b), "cpar"], w=[("cv", c, tb)])
                    if last and tb >= 1:
                        ln_steps[tb - 1]()
            for st in ln_steps[3:]:
                st()
            ar.pop()
            if debug is not None and debug["what"] == "cv":
                dump(cv.rearrange("p k t -> p (k t)"), CVKEYS, KT * S)
            if "MERGE" in enabled:
                merge_fill(0)
                merge_fill(1)
            ar.pop()
            ar.push()
        if "CONV" not in enabled:
            ar.pop()
            ar.push()
            f8 = [sb(512 * 4, F32) for _ in range(8)]
        if debug is not None and debug["what"] == "zc":
            dump(cv.rearrange("p k t -> p (k t)"), CVKEYS, KT * S)

        mT = zT
        MKEYS = [("mT", k, tb) for k in range(KT) for tb in range(4)]
        if "MERGE" in enabled:
            mmbanks[0] = [0, 1, 2, 3, 4, 5]
            sgA, sgB, m1, m2 = f8[0:2], f8[2:4], f8[4:6], f8[6:8]
            cnt = 0
            for i in range(4):
                slot = i % 2
                if i >= 2:
                    merge_fill(i)
                sk = ("slot", slot)
                W = slots[slot]
                for c in range(2):
                    ct = 2 * i + c
                    for tb in range(4):
                        tsl = slice(tb * 512, (tb + 1) * 512)
                        j = cnt % 2
                        cnt += 1
                        bYA = mmbank()
                        mm_fm(ps(bYA), bYA, lambda k: W[:, k, c * 128:(c + 1) * 128], [sk],
                              lambda k: og[:, k, tsl], [("og", hh, tb) for hh in range(H)])
                        bGA = mmbank()
                        mm_fm(ps(bGA), bGA, lambda k: W[:, k, 256 + c * 128:256 + (c + 1) * 128], [sk],
                              lambda k: hT[:, k, tsl], hkeys(tb))
                        sc.op("act", lambda e, bGA=bGA, j=j: e.activation(out=sgA[j], in_=ps(bGA), func=AF.Sigmoid),
                              r=[("ps", bGA)], w=[("f8", j)])
                        sc.op("dve", lambda e, bYA=bYA, j=j: e.tensor_tensor(out=m1[j], in0=ps(bYA), in1=sgA[j], op=ALU.mult),
                              r=[("ps", bYA), ("f8", j)], w=[("f8", 4 + j)])
                        bYB = mmbank()
                        mm_fm(ps(bYB), bYB, lambda k: W[:, k, 512 + c * 128:512 + (c + 1) * 128], [sk],
                              lambda k: cv[:, k, tsl], [("cv", kk, tb) for kk in range(KT)])
                        bGB = mmbank()
                        mm_fm(ps(bGB), bGB, lambda k: W[:, k, 768 + c * 128:768 + (c + 1) * 128], [sk],
                              lambda k: hT[:, k, tsl], hkeys(tb))
                        sc.op("act", lambda e, bGB=bGB, j=j: e.activation(out=sgB[j], in_=ps(bGB), func=AF.Sigmoid),
                              r=[("ps", bGB)], w=[("f8", 2 + j)])
                        sc.op("dve", lambda e, bYB=bYB, j=j: e.tensor_tensor(out=m2[j], in0=ps(bYB), in1=sgB[j], op=ALU.mult),
                              r=[("ps", bYB), ("f8", 2 + j)], w=[("f8", 6 + j)])
                        sc.op("dve", lambda e, j=j, ct=ct, tsl=tsl: e.tensor_tensor(out=mT[:, ct, tsl], in0=m1[j], in1=m2[j], op=ALU.add),
                              r=[("f8", 4 + j), ("f8", 6 + j)], w=[("mT", ct, tb), ("zT", ct)])
            ar.pop()
        if debug is not None and debug["what"] == "mT":
            for c in range(KT):
                dump(mT[:, c, 0:S], [("mT", c, tb) for tb in range(4)], S)

        def make_epilogue(gbuf, tag, resid_of, out_of, ntmp=2):
            junk = sb(D * 2, BF16)
            tmp = [sb(D * 4, F32) for _ in range(ntmp)]
            ss = sb(NT * 4, F32)
            lnv = sb(NT * 4, F32)
            rstd = sb(NT * 4, F32)

            def pair(t):
                p = t % 3
                return psum[:, 2 * p:2 * p + 2, :].rearrange("p b n -> p (b n)"), [("ps", 2 * p), ("ps", 2 * p + 1)]

            def E1(t):
                pa, pk = pair(t)
                sc.op("act", lambda e: e.activation(out=junk, in_=pa, func=AF.Square, accum_out=ss[:, t:t + 1]),
                      r=pk, w=[tag + "junk", (tag, "ss", t)], vr={pk[0]: (tag, t)})
                sc.op("act", lambda e: e.activation(out=lnv[:, t:t + 1], in_=ss[:, t:t + 1], func=AF.Ln, scale=1.0 / D, bias=EPS),
                      r=[(tag, "ss", t)], w=[(tag, "lnv", t)])
                sc.op("act", lambda e: e.activation(out=rstd[:, t:t + 1], in_=lnv[:, t:t + 1], func=AF.Exp, scale=-0.5),
                      r=[(tag, "lnv", t)], w=[(tag, "rstd", t)])

            def E2(t):
                pa, pk = pair(t)
                res, rkey, rv = resid_of(t)
                outb, okey, ov = out_of(t)
                tb_ = t % ntmp
                sc.op("dve", lambda e: e.scalar_tensor_tensor(out=pa, in0=pa, scalar=rstd[:, t:t + 1], in1=bcg[:, gbuf, :],
                                                              op0=ALU.mult, op1=ALU.mult),
                      r=pk + [(tag, "rstd", t), ("bcg", gbuf)], w=pk, vr={pk[0]: (tag, t)})
                sc.op("dve", lambda e: e.tensor_tensor(out=outb, in0=pa, in1=res, op=ALU.add),
                      r=pk + [rkey], w=[okey], vr=rv, vw=ov)
            return E1, E2

        if "WOUT" in enabled:
            wload(0, [(w_o_d, 0)])
            if "FFN1" in enabled:
                ffn1_fill(0)
            sc.fence()
            ar.pop()
            ar.push()
            zT_keep = sb(KT * ZW * 2, BF16)
            load_gain(1, 0)
            load_gain(2, 1)
            xs = [sb(D * 4, F32) for _ in range(3)]
            x1s = [sb(D * 4, F32) for _ in range(3)]
            E1, E2 = make_epilogue(0, "wo", lambda t: (xs[t % 3], ("xs", t % 3), {("xs", t % 3): t}),
                                   lambda t: (x1s[t % 3], ("x1s", t % 3), {("x1s", t % 3): t}))
            N1, N2, N3, N4 = make_norm(1, hT, "h2", lambda t: ("h", "T", t),
                                       lambda t: (x1s[t % 3], ("x1s", t % 3), {("x1s", t % 3): t}))
            Wo = slots[0]

            def W0(t):
                p = t % 3
                for hf in range(2):
                    bk = 2 * p + hf
                    for k in range(KT):
                        sc.op("pe", lambda e, k=k: e.matmul(ps(bk), mT[:, k, t * 128:(t + 1) * 128], Wo[:, k, hf * 512:(hf + 1) * 512],
                                                            start=(k == 0), stop=(k == KT - 1)),
                              r=[("mT", k, t // 4), ("slot", 0)], w=[("ps", bk)], vw={("ps", bk): ("wo", t)})
                sc.op("sp", lambda e: e.dma_start(out=xs[t % 3], in_=x_d[t * 128:(t + 1) * 128, :]),
                      w=[("xs", t % 3)], dma=f"xs{t % 3}", vw={("xs", t % 3): t})

            def W3(t):
                sc.op("sp", lambda e: e.dma_start(out=x1_d[t * 128:(t + 1) * 128, :], in_=x1s[t % 3]),
                      r=[("x1s", t % 3)], w=[("x1d", t)], dma=f"x1st{t % 3}", vr={("x1s", t % 3): t})
                N1(t)

            pipeline(NT, [(0, W0), (1, E1), (2, E2), (3, W3), (4, N2), (5, N3), (6, N4)])
            if debug is not None and debug["what"] == "h2T":
                dump(hT.rearrange("p k t -> p (k t)"), HKEYS, KT * S)
            ar.pop()
        else:
            ar.pop()

        if "FFN1" in enabled:
            sc.fence()
            ar.off = base_mark
            aT = sb(FKT * S * 2, BF16, "p (k t) -> p k t", k=FKT)
            if "FFN2" in enabled:
                sc.op("pool", lambda e: e.dma_start(out=wbig[:, 16:FKT, :],
                                                    in_=w_f2_d[16 * 128:FKT * 128, :].rearrange("(k p) c -> p k c", p=128)),
                      w=[("wfo", 2)], dma="wfo2")
            AKEYS = lambda tb: [("aT", k, tb) for k in range(FKT)]
            sg = [sb(512 * 4, F32) for _ in range(2)]
            cnt = 0
            nfill = (FKT + 3) // 4
            for i in range(nfill):
                nt_ = min(4, FKT - 4 * i)
                slot = (i + 1) % 2
                if i >= 1:
                    ffn1_fill(i)
                sk = ("slot", slot)
                W = slots[slot]
                for c in range(nt_):
                    ct = 4 * i + c
                    for tb in range(4):
                        tsl = slice(tb * 512, (tb + 1) * 512)
                        j = cnt % 2
                        cnt += 1
                        bG = mmbank()
                        mm_fm(ps(bG), bG, lambda k: W[:, k, c * 128:(c + 1) * 128], [sk], lambda k: hT[:, k, tsl], hkeys(tb))
                        bU = mmbank()
                        mm_fm(ps(bU), bU, lambda k: W[:, k, 512 + c * 128:512 + (c + 1) * 128], [sk], lambda k: hT[:, k, tsl], hkeys(tb))
                        sc.op("act", lambda e, bG=bG, j=j: e.activation(out=sg[j], in_=ps(bG), func=AF.Silu),
                              r=[("ps", bG)], w=[("sg", j)])
                        sc.op("dve", lambda e, bU=bU, j=j, ct=ct, tsl=tsl: e.tensor_tensor(out=aT[:, ct, tsl], in0=ps(bU), in1=sg[j], op=ALU.mult),
                              r=[("ps", bU), ("sg", j)], w=[("aT", ct, tb)])
            if debug is not None and debug["what"] == "aT":
                for k in range(FKT):
                    dump(aT[:, k, :], [("aT", k, tb) for tb in range(4)], S)

        if "FFN2" in enabled:
            ar.push()
            sc.op("pool", lambda e: e.dma_start(out=wbig[:, 8:16, :], in_=w_f2_d[8 * 128:16 * 128, :].rearrange("(k p) c -> p k c", p=128)),
                  w=[("wfo", 1), ("slot", 1)], dma="wfo1")
            sc.op("pool", lambda e: e.dma_start(out=wbig[:, 0:8, :], in_=w_f2_d[0:8 * 128, :].rearrange("(k p) c -> p k c", p=128)),
                  w=[("wfo", 0), ("slot", 0)], dma="wfo0")
            load_gain(3, 0)
            xs2 = [sb(D * 4, F32) for _ in range(2)]
            osb = [sb(D * 4, F32) for _ in range(2)]
            E1, E2 = make_epilogue(0, "fo", lambda t: (xs2[t % 2], ("xs2", t % 2), {("xs2", t % 2): t}),
                                   lambda t: (osb[t % 2], ("osb", t % 2), {("osb", t % 2): t}), ntmp=2)

            korder = list(range(16, FKT)) + list(range(8, 16)) + list(range(8))

            def F0_part(t, k_lo, k_hi):
                p = t % 3
                for hf in range(2):
                    bk = 2 * p + hf
                    for ki in range(k_lo, k_hi):
                        k = korder[ki]
                        sc.op("pe", lambda e, k=k, ki=ki: e.matmul(ps(bk), aT[:, k, t * 128:(t + 1) * 128], wbig[:, k, hf * 512:(hf + 1) * 512],
                                                                   start=(ki == 0), stop=(ki == FKT - 1)),
                              r=[("aT", k, t // 4), ("wfo", k // 8)], w=[("ps", bk)], vw={("ps", bk): ("fo", t)})

            for t0 in range(3):
                F0_part(t0, 0, FKT - 8)

            def F0(t):
                if t < 3:
                    F0_part(t, FKT - 8, FKT)
                else:
                    F0_part(t, 0, FKT)
                sc.op("sp", lambda e: e.dma_start(out=xs2[t % 2], in_=x1_d[t * 128:(t + 1) * 128, :]),
                      r=[("x1d", t)], w=[("xs2", t % 2)], dma=f"xs2_{t % 2}", vw={("xs2", t % 2): t})

            def F3(t):
                sc.op("sp", lambda e: e.dma_start(out=out_d[t * 128:(t + 1) * 128, :], in_=osb[t % 2]),
                      r=[("osb", t % 2)], dma=f"ost{t % 2}", vr={("osb", t % 2): t})

            pipeline(NT, [(0, F0), (1, E1), (2, E2), (3, F3)])
            ar.pop()

        def finish():
            dma_keys = sorted(sc.dma_count.keys())
            sem_objs = {}
            for e in ENGS:
                sem_objs[e] = es.enter_context(nc.semaphore("s_" + e))
            dsem = {}
            for k in dma_keys:
                dsem[k] = es.enter_context(nc.semaphore("d_" + k))
            block = es.enter_context(nc.Block())
            sc.emit(nc, sem_objs, dsem, block)

        finish()
    return nc


def host_prep(inputs):
    g = np.stack([inputs["norm_mix_pre"][0], inputs["norm_mix_post"][0], inputs["norm_ffn_pre"][0],
                  inputs["norm_ffn_post"][0], inputs["gla_norm"][0]], axis=0).reshape(1, 5 * D)
    bc = np.ascontiguousarray(np.broadcast_to(g, (128, 5 * D))).astype(np.float32)
    cp = np.zeros((128, KT, 34), np.float32)
    cp[:, :, 0:31] = inputs["conv_w"][0].T.reshape(KT, 128, CK).transpose(1, 0, 2)
    cp[:, :, 31] = inputs["conv_b"][0].reshape(KT, 128).T
    cp[:, :, 32] = inputs["conv_ln_g"][0].reshape(KT, 128).T
    cp[:, :, 33] = inputs["conv_ln_b"][0].reshape(KT, 128).T
    wup = np.zeros((32, 512), np.float32)
    wup[0:16] = inputs["w_alpha_up"][0]
    wup[16] = inputs["b_alpha"][0]
    cst = np.triu(np.ones((128, 128), np.float32))
    shared = {"w_in": np.ascontiguousarray(inputs["w_in"][0]),
              "w_gla_out": np.ascontiguousarray(inputs["w_gla_out"][0]),
              "w_conv_out": np.ascontiguousarray(inputs["w_conv_out"][0]),
              "w_out": np.ascontiguousarray(inputs["w_out"][0]),
              "w_ffn_in": np.ascontiguousarray(inputs["w_ffn_in"][0]),
              "w_ffn_out": np.ascontiguousarray(inputs["w_ffn_out"][0]),
              "bc": bc, "cpar": np.ascontiguousarray(cp.reshape(128, KT * 34)), "wup": wup, "cst": cst}
    return shared


def kernel(_debug=None, _stop=None, **inputs):
    inputs = {k: np.asarray(v) for k, v in inputs.items()}
    shared = host_prep(inputs)
    nc = build_nc(_debug, _stop)
    in_maps = []
    for c in range(8):
        m = dict(shared)
        m["x"] = np.ascontiguousarray(inputs["x"][c])
        in_maps.append(m)
    res = run_bass_kernel_spmd(nc, in_maps, core_ids=list(range(8)))
    if _debug is not None:
        return [r["dbg"] for r in res.results]
    return np.stack([r["out"] for r in res.results], axis=0).astype(np.float32)
```
